# Optimizing a Trainium2 kernel written in Bass

```python
import math
import jax, jax.numpy as jnp
from jax import lax
import numpy as np

D_MODEL = 1024
BATCH = 16
SEQ = 4096
DEPTH = 2

D_SSD = D_MODEL
SSD_HEADDIM = 64
SSD_HEADS = D_SSD // SSD_HEADDIM
SSD_GROUPS = 4
HEADS_PER_GROUP = SSD_HEADS // SSD_GROUPS
SSD_STATE = 128
CHUNK = 128
CONV_WIDTH = 3
CONV_XBC = D_SSD + 2 * SSD_GROUPS * SSD_STATE
D_SC = D_MODEL
SC_GROUPS = 16
D_MIX = D_SSD + D_SC
D_FF = ((8 * D_MODEL // 3 + 255) // 256) * 256
RMS_EPS = 1e-5
DT_MIN = 1e-3
DT_MAX = 1e-1
IN_COLS = D_SSD + CONV_XBC + 2 * SSD_HEADS + 3 * D_SC
SPLIT_POINTS = (D_SSD,
                D_SSD + CONV_XBC,
                D_SSD + CONV_XBC + SSD_HEADS,
                D_SSD + CONV_XBC + 2 * SSD_HEADS,
                D_SSD + CONV_XBC + 2 * SSD_HEADS + D_SC,
                D_SSD + CONV_XBC + 2 * SSD_HEADS + 2 * D_SC)

kernel_name = 'bidir_hybrid_ssd_shortconv_convffn'


def rms_norm(x, w):
    xf = x.astype(jnp.float32)
    y = xf * lax.rsqrt(jnp.mean(xf * xf, axis=-1, keepdims=True) + RMS_EPS)
    return (y * w.astype(jnp.float32)).astype(x.dtype)


def dwconv_centred(u, w, b=None):
    k = w.shape[0]
    half = k // 2
    length = u.shape[1]
    up = jnp.pad(u, ((0, 0), (half, half), (0, 0)))
    out = up[:, 0:length] * w[0]
    for i in range(1, k):
        out = out + up[:, i:i + length] * w[i]
    if b is not None:
        out = out + b
    return out


def ssd_chunked(xdt, a, bm, cm):
    bsz, length, g, r, p = xdt.shape
    n = bm.shape[-1]
    c = length // CHUNK
    xdt = xdt.reshape(bsz, c, CHUNK, g, r, p)
    a = a.reshape(bsz, c, CHUNK, g, r)
    bm = bm.reshape(bsz, c, CHUNK, g, n)
    cm = cm.reshape(bsz, c, CHUNK, g, n)
    a_cs = jnp.cumsum(a, axis=2)
    seg = a_cs[:, :, :, None] - a_cs[:, :, None, :]
    mask = jnp.tril(jnp.ones((CHUNK, CHUNK), dtype=bool))[:, :, None, None]
    decay_ls = jnp.exp(jnp.where(mask, seg, -jnp.inf))
    scores = jnp.einsum('bclgn,bcsgn->bclsg', cm, bm)
    y_diag = jnp.einsum('bclsgr,bcsgrp->bclgrp', scores[..., None] * decay_ls, xdt)
    decay_to_end = jnp.exp(a_cs[:, :, -1:] - a_cs)
    states = jnp.einsum('bclgn,bclgrp->bcgrpn', bm, xdt * decay_to_end[..., None])
    chunk_decay = jnp.exp(a_cs[:, :, -1])

    def step(hstate, inp):
        s_c, d_c = inp
        return hstate * d_c[..., None, None] + s_c, hstate

    h0 = jnp.zeros((bsz, g, r, p, n), dtype=xdt.dtype)
    _, h_prev = lax.scan(step, h0, (jnp.moveaxis(states, 1, 0), jnp.moveaxis(chunk_decay, 1, 0)))
    h_prev = jnp.moveaxis(h_prev, 0, 1)
    y_off = jnp.einsum('bclgn,bcgrpn->bclgrp', cm, h_prev) * jnp.exp(a_cs)[..., None]
    return (y_diag + y_off).reshape(bsz, length, g, r, p)


def ssd_branch(z, xbc, dt_f_raw, dt_b_raw, conv_w, conv_b, dt_bias_f, dt_bias_b,
               a_log_f, a_log_b, d_skip, norm_w):
    f32 = jnp.float32
    bsz, length, _ = z.shape
    xbc = jax.nn.silu(dwconv_centred(xbc, conv_w, conv_b))
    xs, bm, cm = jnp.split(xbc, [D_SSD, D_SSD + SSD_GROUPS * SSD_STATE], axis=-1)
    xs = xs.astype(f32).reshape(bsz, length, SSD_GROUPS, HEADS_PER_GROUP, SSD_HEADDIM)
    bm = bm.astype(f32).reshape(bsz, length, SSD_GROUPS, SSD_STATE)
    cm = cm.astype(f32).reshape(bsz, length, SSD_GROUPS, SSD_STATE)

    def discretise(dt_raw, dt_bias, a_log):
        dt = jax.nn.softplus(dt_raw.astype(f32) + dt_bias.astype(f32))
        dt = dt.reshape(bsz, length, SSD_GROUPS, HEADS_PER_GROUP)
        a = -jnp.exp(a_log.astype(f32)).reshape(SSD_GROUPS, HEADS_PER_GROUP) * dt
        return xs * dt[..., None], a

    xdt_f, a_f = discretise(dt_f_raw, dt_bias_f, a_log_f)
    xdt_b, a_b = discretise(dt_b_raw, dt_bias_b, a_log_b)
    flip = lambda t: jnp.flip(t, axis=1)
    y_f = ssd_chunked(xdt_f, a_f, bm, cm)
    y_b = flip(ssd_chunked(flip(xdt_b), flip(a_b), flip(bm), flip(cm)))
    y = y_f + y_b + xs * d_skip.astype(f32).reshape(SSD_GROUPS, HEADS_PER_GROUP)[..., None]
    y = y.reshape(bsz, length, D_SSD) * jax.nn.silu(z.astype(f32))
    yg = y.reshape(bsz, length, SSD_GROUPS, D_SSD // SSD_GROUPS)
    yg = yg * lax.rsqrt(jnp.mean(yg * yg, axis=-1, keepdims=True) + RMS_EPS)
    return (yg.reshape(bsz, length, D_SSD) * norm_w.astype(f32)).astype(z.dtype)


def short_conv_branch(sc_b, sc_c, sc_v, conv_w):
    return sc_b * dwconv_centred(sc_c * sc_v, conv_w)


def conv_ffn(n, w_up, conv_w, conv_b, w_down):
    u = jnp.einsum('bld,df->blf', n, w_up)
    u = dwconv_centred(u, conv_w, conv_b)
    gate, up = jnp.split(u, 2, axis=-1)
    return jnp.einsum('blf,fd->bld', jax.nn.silu(gate) * up, w_down)


def _dt_bias_init(k, shape):
    u = jax.random.uniform(k, shape, jnp.float32)
    dt = jnp.exp(u * (math.log(DT_MAX) - math.log(DT_MIN)) + math.log(DT_MIN))
    return dt + jnp.log(-jnp.expm1(-dt))


def setup_inputs(seed: int = 0) -> dict:
    key = jax.random.key(seed)
    ks = jax.random.split(key, 20)
    f32 = jnp.float32
    nrm = lambda k, shape, scale: jax.random.normal(k, shape, f32) * scale
    gain = lambda k, shape: 1.0 + 0.01 * jax.random.normal(k, shape, f32)
    return {
        'x': jax.random.normal(ks[0], (BATCH, SEQ, D_MODEL), f32),
        'w_in': nrm(ks[1], (DEPTH, D_MODEL, IN_COLS), D_MODEL ** -0.5),
        'conv_xbc_w': nrm(ks[2], (DEPTH, CONV_WIDTH, CONV_XBC), CONV_WIDTH ** -0.5),
        'conv_xbc_b': nrm(ks[3], (DEPTH, CONV_XBC), 0.01),
        'dt_bias_f': _dt_bias_init(ks[4], (DEPTH, SSD_HEADS)),
        'dt_bias_b': _dt_bias_init(ks[5], (DEPTH, SSD_HEADS)),
        'a_log_f': jnp.log(jax.random.uniform(ks[6], (DEPTH, SSD_HEADS), f32, 1.0, 16.0)),
        'a_log_b': jnp.log(jax.random.uniform(ks[7], (DEPTH, SSD_HEADS), f32, 1.0, 16.0)),
        'd_skip': gain(ks[8], (DEPTH, SSD_HEADS)),
        'ssd_norm_w': gain(ks[9], (DEPTH, D_SSD)),
        'sc_conv_w': nrm(ks[10], (DEPTH, CONV_WIDTH, D_SC), CONV_WIDTH ** -0.5),
        'w_out': nrm(ks[11], (DEPTH, D_MIX, D_MODEL), D_MIX ** -0.5),
        'norm1_w': gain(ks[12], (DEPTH, D_MODEL)),
        'norm2_w': gain(ks[13], (DEPTH, D_MODEL)),
        'w_ffn_up': nrm(ks[14], (DEPTH, D_MODEL, 2 * D_FF), D_MODEL ** -0.5),
        'ffn_conv_w': nrm(ks[15], (DEPTH, CONV_WIDTH, 2 * D_FF), CONV_WIDTH ** -0.5),
        'ffn_conv_b': nrm(ks[16], (DEPTH, 2 * D_FF), 0.01),
        'w_ffn_down': nrm(ks[17], (DEPTH, D_FF, D_MODEL), D_FF ** -0.5),
        'final_norm_w': gain(ks[18], (D_MODEL,)),
    }


def reference(x, w_in, conv_xbc_w, conv_xbc_b, dt_bias_f, dt_bias_b, a_log_f, a_log_b,
              d_skip, ssd_norm_w, sc_conv_w, w_out, norm1_w, norm2_w, w_ffn_up,
              ffn_conv_w, ffn_conv_b, w_ffn_down, final_norm_w):
    h = x
    for l in range(DEPTH):
        n = rms_norm(h, norm1_w[l])
        proj = jnp.einsum('bld,de->ble', n, w_in[l])
        z, xbc, dt_f, dt_b, sc_b, sc_c, sc_v = jnp.split(proj, SPLIT_POINTS, axis=-1)
        y_ssd = ssd_branch(z, xbc, dt_f, dt_b, conv_xbc_w[l], conv_xbc_b[l],
                           dt_bias_f[l], dt_bias_b[l], a_log_f[l], a_log_b[l],
                           d_skip[l], ssd_norm_w[l])
        y_sc = short_conv_branch(sc_b, sc_c, sc_v, sc_conv_w[l])
        mix = jnp.concatenate([y_ssd, y_sc], axis=-1)
        h = h + jnp.einsum('ble,ed->bld', mix, w_out[l])
        n = rms_norm(h, norm2_w[l])
        h = h + conv_ffn(n, w_ffn_up[l], ffn_conv_w[l], ffn_conv_b[l], w_ffn_down[l])
    return rms_norm(h, final_norm_w)
```

```python
import numpy as np
import concourse.bass as bass
import concourse.mybir as mybir
from concourse.bass_utils import run_bass_kernel_spmd

F32 = mybir.dt.float32
BF16 = mybir.dt.bfloat16
ALU = mybir.AluOpType
AF = mybir.ActivationFunctionType

D = 1024
KD = 8
NH = 16
HP = 64
NG = 4
KF = 22
NFI = 40
NFU = 44
EPS = 1e-5
ZW = 1056
OFF_ZDT = 0
OFF_WO = OFF_ZDT + KD * ZW
OFF_WI = OFF_WO + 16 * 1024
OFF_WU = OFF_WI + NFI * 1024
OFF_WD = OFF_WU + NFU * 1024
XT = OFF_WD + KF * 1024
PV_CWX, PV_CBX, PV_CWS, PV_CWF, PV_CBF, PV_N1, PV_N2, NPV = 0, 48, 64, 88, 220, 264, 272, 280
BV_NW, BV_D, BV_DTB, BV_ALOG, NBV = 0, 1024, 1040, 1072, 1104
TCH = 3


class T:
    __slots__ = ("name", "w", "r")

    def __init__(self, name):
        self.name = name
        self.w = {}
        self.r = {}


class Ctx:
    CE = ("pe", "act", "dve", "pool")

    def __init__(self, nc, ndma_sems=16):
        self.nc = nc
        self.eng = {"pe": nc.tensor, "act": nc.scalar, "dve": nc.vector, "pool": nc.gpsimd, "sp": nc.sync}
        self.sem = {e: nc.alloc_semaphore("s_" + e) for e in self.CE}
        self.cnt = {e: 0 for e in self.CE}
        self.waited = {e: {} for e in self.eng}
        self.dsem, self.dcnt, self.drr = {}, {}, {}
        for q in ("sp", "pool"):
            self.dsem[q] = [nc.alloc_semaphore("d_%s%d" % (q, i)) for i in range(ndma_sems)]
            self.dcnt[q] = [0] * ndma_sems
            self.drr[q] = 0
        self.nwaits = 0
        self.nins = 0

    def _wait(self, e, ev):
        sem, val = ev
        key = id(sem)
        if self.waited[e].get(key, 0) >= val:
            return
        self.waited[e][key] = val
        self.eng[e].wait_ge(sem, val)
        self.nwaits += 1

    def _deps(self, e, reads, writes, part=False):
        mysem = self.sem.get(e)
        for t in reads:
            for ev in t.w.values():
                self._wait(e, ev)
        for t in writes:
            if not part:
                for ev in t.w.values():
                    self._wait(e, ev)
            for ev in t.r.values():
                self._wait(e, ev)

    def _mark(self, ev, reads, writes, part=False):
        k = id(ev[0])
        for t in reads:
            t.r[k] = ev
        for t in writes:
            if part:
                t.w[k] = ev
            else:
                t.w = {k: ev}
            t.r = {}

    def op(self, e, reads, writes, emit):
        self._deps(e, reads, writes)
        ins = emit(self.eng[e])
        self.cnt[e] += 1
        ins.then_inc(self.sem[e], 1)
        self._mark((self.sem[e], self.cnt[e]), reads, writes)
        self.nins += 1
        return ins

    def dma(self, q, out, in_, reads, writes, part=False, slow=False):
        self._deps(q, reads, writes, part)
        k = self.drr[q]
        self.drr[q] = (k + 1) % len(self.dsem[q])
        sem = self.dsem[q][k]
        if self.dcnt[q][k] > 0:
            self._wait(q, (sem, self.dcnt[q][k]))
        self.dcnt[q][k] += 16
        if slow:
            self.eng[q].dma_start(out=out, in_=in_, allow_slow_non_contiguous=True).then_inc(sem, 16)
        else:
            self.eng[q].dma_start(out=out, in_=in_).then_inc(sem, 16)
        self._mark((sem, self.dcnt[q][k]), reads, writes, part)
        self.nins += 1

    def inherit(self, new, olds):
        for o in olds:
            for k, ev in list(o.w.items()) + list(o.r.items()):
                if k not in new.r or new.r[k][1] < ev[1]:
                    new.r[k] = ev

    def act(self, reads, writes, **kw):
        return self.op("act", reads, writes, lambda e: e.activation(**kw))

    def tt(self, eng, reads, writes, out, in0, in1, op):
        return self.op(eng, reads, writes, lambda e: e.tensor_tensor(out=out, in0=in0, in1=in1, op=op))

    def ts(self, eng, reads, writes, out, in0, s1, op0, s2=None, op1=None):
        if op1 is None:
            return self.op(eng, reads, writes, lambda e: e.tensor_single_scalar(out=out, in_=in0, scalar=s1, op=op0))
        return self.op(eng, reads, writes, lambda e: e.tensor_scalar(out=out, in0=in0, scalar1=s1, scalar2=s2, op0=op0, op1=op1))

    def stt(self, reads, writes, out, in0, scalar, in1, op0, op1):
        return self.op("dve", reads, writes, lambda e: e.scalar_tensor_tensor(out=out, in0=in0, scalar=scalar, in1=in1, op0=op0, op1=op1))

    def copy(self, eng, reads, writes, out, in_):
        if eng == "act":
            return self.act(reads, writes, out=out, in_=in_, func=AF.Copy)
        return self.op(eng, reads, writes, lambda e: e.tensor_copy(out=out, in_=in_))

    def mm(self, reads, writes, items):
        def emit(e):
            ins = None
            for (o, l, r, st, sp) in items:
                ins = e.matmul(o, lhsT=l, rhs=r, start=st, stop=sp)
            return ins
        return self.op("pe", reads, writes, emit)

    def tr(self, reads, writes, items, ident):
        def emit(e):
            ins = None
            for (o, i) in items:
                ins = e.transpose(out=o, in_=i, identity=ident)
            return ins
        return self.op("pe", reads, writes, emit)


def bc3(ap, n_outer, n_inner):
    return ap.unsqueeze(2).to_broadcast([128, n_outer, n_inner])


def build(L, NSEQ, DEPTH, dbg=False):
    NCH = L // 128
    tiles = [(c0, min(TCH, NCH - c0)) for c0 in range(0, NCH, TCH)]
    nc = bass.Bass("TRN2", target_bir_lowering=False)
    c = Ctx(nc)
    skind = "ExternalOutput" if dbg else "Internal"

    def din(name, shape):
        return nc.dram_tensor(name, list(shape), F32, kind="ExternalInput").ap()

    def dscr(name, shape, dt):
        return nc.dram_tensor(name, list(shape), dt, kind=skind).ap()

    x_d = din("x", [NSEQ, L, D])
    wall_d = din("wall", [DEPTH, 128, XT])
    pvec_d = din("pvec", [DEPTH, 128, NPV])
    bvec_t = nc.dram_tensor("bvec", [DEPTH, NBV], F32, kind="ExternalInput")
    fnw_t = nc.dram_tensor("fnw", [1, D], F32, kind="ExternalInput")
    consts_d = din("consts", [128, 768])
    sel_d = din("sel", [128, 2048])
    out_d = nc.dram_tensor("out", [NSEQ, L, D], F32, kind="ExternalOutput").ap()

    wbf = dscr("wbf", [DEPTH, 128, XT], BF16)
    nT_s = [[dscr("nT_%d_%d" % (l, s), [128, KD, L + 2], BF16) for s in range(NSEQ)] for l in range(DEPTH)]
    n2T_s = [[dscr("n2T_%d_%d" % (l, s), [128, KD, L + 2], BF16) for s in range(NSEQ)] for l in range(DEPTH)]

    def per_chunk(name, shape, dt):
        return [[dscr("%s_%d_%d" % (name, l, s), [NCH] + list(shape), dt) for s in range(NSEQ)] for l in range(DEPTH)]

    yp_s = per_chunk("yp", [128, D], F32)
    xt_s = per_chunk("xt", [128, D], BF16)
    bt_s = per_chunk("bt", [128, 512], BF16)
    BT_s = per_chunk("BT", [128, 512], BF16)
    CT_s = per_chunk("CT", [128, 512], BF16)
    sz_s = per_chunk("sz", [128, D], BF16)
    dt_s = per_chunk("dt", [128, 32], F32)
    hp_s = per_chunk("hp", [128, D], F32)
    hm_s = per_chunk("hm", [128, D], F32)
    ho_s = [[dscr("ho_%d_%d" % (l, s), [NCH, 128, D], F32) for s in range(NSEQ)] for l in range(DEPTH - 1)]

    def mkT(name, *dims):
        if not dims:
            return T(name)
        return [mkT("%s_%d" % (name, i), *dims[1:]) for i in range(dims[0])]

    T_x = T("x")
    T_wseg = mkT("wseg", DEPTH, 5)
    T_nT = mkT("nT", DEPTH, NSEQ, NCH)
    T_nTb = mkT("nTb", DEPTH, NSEQ)
    T_n2T = mkT("n2T", DEPTH, NSEQ, NCH)
    T_n2Tb = mkT("n2Tb", DEPTH, NSEQ)
    T_scr = {nm: mkT(nm, DEPTH, NSEQ, NCH) for nm in ("yp", "xt", "bt", "BT", "CT", "sz", "dt", "hp", "hm", "ho")}
    T_out = mkT("out", NSEQ, NCH)

    def sb(name, shape, dt=F32):
        return nc.alloc_sbuf_tensor("s_" + name, list(shape), dt).ap()

    consts = sb("consts", [128, 768]); T_const = T("consts")
    sel = sb("sel", [128, 2048])
    idb = sb("idb", [128, 128], BF16); T_idb = T("idb")
    ident_f = consts[:, 0:128]
    Uincl = consts[:, 128:256]
    Ustrict = consts[:, 256:384]
    maskF = consts[:, 384:512]
    maskB = consts[:, 512:640]
    Sel127 = consts[:, 640:768]
    pv = [sb("pv%d" % l, [128, NPV]) for l in range(DEPTH)]
    bv = [sb("bv%d" % l, [128, NBV]) for l in range(DEPTH)]
    T_pv = T("pv")
    Abc = [sb("Abc%d" % l, [128, 32]) for l in range(DEPTH)]; T_A = T("Abc")
    fnw = sb("fnw", [128, D])
    zcol = sb("zcol", [128, KD, 1], BF16); T_zcol = T("zcol")

    wA = sb("wA", [128, KF * 1024], BF16); T_wA = T("wA")
    wO = sb("wO", [128, 8 * 1024], BF16); T_wO = T("wO")
    NRING = 6
    ring = [sb("ring%d" % i, [128, KD, 128], BF16) for i in range(NRING)]; T_ring = mkT("ring", NRING)
    WMAX = TCH * 128 + 2
    nTw = [sb("nTw%d" % i, [128, KD, WMAX], BF16) for i in range(2)]; T_nTw = mkT("nTw", 2)
    big = sb("big", [128, 24 * 384], BF16)
    xbc = big[:, 0:16 * 384].rearrange("p (a b) -> p a b", a=16)
    ysc = big[:, 16 * 384:24 * 384].rearrange("p (a b) -> p a b", a=8)
    abuf = big[:, 0:KF * 384].rearrange("p (a b) -> p a b", a=KF)
    T_xbc = mkT("xbc", 16); T_ysc = mkT("ysc", 8); T_a = mkT("a", KF)
    tokf = [sb("tokf%d" % i, [128, D]) for i in range(6)]; T_tokf = mkT("tokf", 6)
    tokb = [sb("tokb%d" % i, [128, D], BF16) for i in range(8)]; T_tokb = mkT("tokb", 8)
    btok = [sb("btok%d" % i, [128, 512], BF16) for i in range(2)]; T_btok = mkT("btok", 2)
    BTb = [sb("BTb%d" % i, [128, 512], BF16) for i in range(2)]; T_BTb = mkT("BTb", 2)
    CTb = [sb("CTb%d" % i, [128, 512], BF16) for i in range(2)]; T_CTb = mkT("CTb", 2)
    Gm = sb("Gm", [128, 512]); T_Gm = T("Gm")
    seg = sb("seg", [128, 512]); T_seg = T("seg")
    Eb = sb("Eb", [128, 512]); T_E = T("E")
    MT = sb("MT", [128, NH * 128], BF16); T_MT = mkT("MT", 4)
    accA = sb("accA", [128, 384]); T_accA = T("accA")
    accB = sb("accB", [128, 384]); T_accB = T("accB")
    cbuf = sb("cbuf", [128, WMAX]); T_cbuf = T("cbuf")
    cvb = sb("cvb", [128, WMAX]); T_cv = T("cv")
    sgb = sb("sgb", [128, 384]); T_sg = T("sg")
    accC = sb("accC", [128, 384]); T_accC = T("accC")
    accD = sb("accD", [128, 384]); T_accD = T("accD")
    sgb2 = sb("sgb2", [128, 384]); T_sg2 = T("sg2")
    nTst = sb("nTst", [128, KD, 128], BF16); T_nTst = T("nTst")
    ynT = sb("ynT", [128, KD, 128], BF16); T_ynT = T("ynT")
    sm = sb("sm", [128, 16, 32])
    T_sm = mkT("sm", 16)
    dts = [sb("dts%d" % i, [128, 32]) for i in range(2)]; T_dts = mkT("dts", 2)
    qTs = sb("qTs", [128, 128]); T_qTs = T("qTs")
    ss = sb("ss", [128, 8]); T_ss = T("ss")
    rs = sb("rs", [128, 8]); T_rs = T("rs")

    def ps(name, shape, dt=F32):
        return nc.alloc_psum_tensor("p_" + name, list(shape), dt).ap()

    proj = [ps("proj%d" % i, [128, 512]) for i in range(2)]; T_proj = mkT("proj", 2)
    pmix = ps("pmix", [128, 512])
    T_pmix = T("pmix")
    psm = pmix[:, 0:256]; T_psm = [T_pmix] * 4
    pbtr = pmix[:, 256:512].bitcast(BF16); T_pbtr = T_pmix
    assert tuple(pbtr.shape) == (128, 512), pbtr.shape
    ptok = [ps("ptok%d" % i, [128, 512]) for i in range(2)]; T_ptok = mkT("ptok", 2)
    pxtr = ps("pxtr", [128, 1024], BF16); T_pxtr = T("pxtr")
    pgt = ps("pgt", [128, 512]); T_pgt = T("pgt")
    pqbc = ps("pqbc", [128, 512]); T_pqbc = T("pqbc")

    c.dma("sp", consts, consts_d, [], [T_const], part=True)
    c.dma("sp", sel, sel_d, [], [T_const], part=True)
    for l in range(DEPTH):
        c.dma("sp", pv[l], pvec_d[l], [], [T_pv], part=True)
        c.dma("sp", bv[l], bass.AP(bvec_t, l * NBV, [[0, 128], [1, NBV]]), [], [T_pv], part=True)
    c.dma("sp", fnw, bass.AP(fnw_t, 0, [[0, 128], [1, D]]), [], [T_pv], part=True)
    c.copy("dve", [T_const], [T_idb], idb, ident_f)
    c.op("pool", [], [T_zcol], lambda e: e.memset(zcol, 0.0))
    for l in range(DEPTH):
        c.act([T_pv], [T_A], out=Abc[l], in_=bv[l][:, BV_ALOG:BV_ALOG + 32], func=AF.Exp)
        c.ts("dve", [T_A], [T_A], Abc[l], Abc[l], -1.0, ALU.mult)
    segs = [(OFF_ZDT, OFF_WO), (OFF_WO, OFF_WI), (OFF_WI, OFF_WU), (OFF_WU, OFF_WD), (OFF_WD, XT)]
    CB = 4096
    for l in range(DEPTH):
        for si, (a0, a1) in enumerate(segs):
            for b0 in range(a0, a1, CB):
                b1 = min(b0 + CB, a1)
                c.dma("pool", wbf[l, :, b0:b1], wall_d[l, :, b0:b1], [], [T_wseg[l][si]], part=True)
    for l in range(DEPTH):
        for s in range(NSEQ):
            for (buf, Tb) in ((nT_s, T_nTb), (n2T_s, T_n2Tb)):
                c.dma("sp", buf[l][s][:, :, 0:1], zcol, [T_zcol], [Tb[l][s]], part=True, slow=True)
                c.dma("sp", buf[l][s][:, :, L + 1:L + 2], zcol, [T_zcol], [Tb[l][s]], part=True, slow=True)

    ring_i = [0]

    def ring_load(l, si, off):
        i = ring_i[0] % NRING
        ring_i[0] += 1
        c.dma("sp", ring[i].rearrange("p a b -> p (a b)"), wbf[l, :, off:off + 1024], [T_wseg[l][si]], [T_ring[i]])
        return ring[i], T_ring[i]

    T_ss2, T_rs2 = T("ss2"), T("rs2")

    def rstd_from_ss(ncols, inv_n, c0=0, Tss=None, Trs=None):
        Tss = Tss or T_ss
        Trs = Trs or T_rs
        c.act([Tss], [Trs], out=rs[:, c0:c0 + ncols], in_=ss[:, c0:c0 + ncols], func=AF.Ln, scale=inv_n, bias=EPS)
        c.act([Trs], [Trs], out=rs[:, c0:c0 + ncols], in_=rs[:, c0:c0 + ncols], func=AF.Exp, scale=-0.5)

    def norm_to_nT(l, hbuf, Th, nwoff, dst, Tdst, cidx, junk, Tjunk, nbb, Tnb, col=0, Tss=None, Trs=None):
        Tss = Tss or T_ss
        Trs = Trs or T_rs
        c.act([Th], [Tjunk, Tss], out=junk, in_=hbuf, func=AF.Square, accum_out=ss[:, col:col + 1])
        rstd_from_ss(1, 1.0 / D, col, Tss, Trs)
        c.ts("dve", [Th, Trs], [Tnb], nbb, hbuf, rs[:, col:col + 1], ALU.mult)
        c.tr([Tnb, T_idb], [T_pxtr], [(pxtr[:, k * 128:(k + 1) * 128], nbb[:, k * 128:(k + 1) * 128]) for k in range(KD)], idb)
        c.tt("dve", [T_pxtr, T_pv], [T_nTst], nTst, pxtr.rearrange("p (a b) -> p a b", a=KD),
             bc3(pv[l][:, nwoff:nwoff + KD], KD, 128), ALU.mult)
        c.dma("sp", dst[:, :, 1 + cidx * 128:1 + (cidx + 1) * 128], nTst, [T_nTst], [Tdst])

    def P1(l, s):
        for cidx in range(NCH):
            hb, Thb = tokf[2 + cidx % 2], T_tokf[2 + cidx % 2]
            if l == 0:
                c.dma("sp", hb, x_d[s, cidx * 128:(cidx + 1) * 128, :], [T_x], [Thb])
            else:
                c.dma("sp", hb, ho_s[l - 1][s][cidx], [T_scr["ho"][l - 1][s][cidx]], [Thb])
            norm_to_nT(l, hb, Thb, PV_N1, nT_s[l][s], T_nT[l][s][cidx], cidx, tokb[5], T_tokb[5], tokb[6], T_tokb[6],
                       col=4, Tss=T_ss2, Trs=T_rs2)
            yield

    def small_chain(l, dtt, Tdt, d0, strictU):
        A = Abc[l]
        a_, cs_, q_, dte_, scl_, dec_, cdte_, tmp_ = (sm[:, i, :] for i in range(5, 13))
        sl = slice(d0, d0 + 16)
        c.tt("dve", [Tdt, T_A], [T_sm[5]], a_, dtt, A, ALU.mult)
        c.mm([T_sm[5], T_const], [T_psm[1]], [(psm[:, 32:64], Uincl, a_, True, True)])
        c.copy("dve", [T_psm[1]], [T_sm[6]], cs_, psm[:, 32:64])
        c.mm([T_sm[6], T_const], [T_psm[2]], [(psm[:, 64:96], Sel127, cs_, True, True)])
        if d0 == 0:
            alhs = a_[:, 0:16]
            Talhs = T_sm[5]
        else:
            alhs = sm[:, 13, 0:16]
            Talhs = T_sm[13]
            c.copy("dve", [T_sm[5]], [T_sm[13]], alhs, a_[:, sl])
        c.mm([Talhs, T_const], [T_psm[3]], [(psm[0:16, 128:256], alhs, Ustrict if strictU else Uincl, True, True)])
        c.copy("dve", [T_psm[3]], [T_qTs], qTs[0:16, :], psm[0:16, 128:256])
        tot_ = sm[:, 14, :]
        c.copy("dve", [T_psm[2]], [T_sm[14]], tot_, psm[:, 64:96])
        if not strictU:
            c.copy("dve", [T_sm[6]], [T_sm[7]], q_, cs_)
            c.tt("dve", [T_sm[14], T_sm[6]], [T_sm[12]], tmp_, tot_, cs_, ALU.subtract)
            c.act([T_sm[12]], [T_sm[8]], out=dte_, in_=tmp_, func=AF.Exp)
            c.act([T_sm[6]], [T_sm[9]], out=scl_, in_=cs_, func=AF.Exp)
        else:
            import os as _os
            k4 = int(_os.environ.get("KSTOP4", "99"))
            c.tt("dve", [T_sm[6], T_sm[5]], [T_sm[7]], q_, cs_, a_, ALU.subtract)
            c.act([T_sm[7]], [T_sm[8]], out=dte_, in_=q_, func=AF.Exp)
            if k4 < 3: return None, None, None, None
            c.tt("dve", [T_sm[14], T_sm[7]], [T_sm[12]], tmp_, tot_, q_, ALU.subtract)
            if k4 < 4: return None, None, None, None
            c.act([T_sm[12]], [T_sm[9]], out=scl_, in_=tmp_, func=AF.Exp)
            if k4 < 5: return None, None, None, None
        c.act([T_sm[14]], [T_sm[10]], out=dec_, in_=tot_, func=AF.Exp)
        if strictU and k4 < 6: return None, None, None, None
        c.tt("dve", [Tdt, T_sm[8]], [T_sm[11]], cdte_, dtt, dte_, ALU.mult)
        return q_[:, sl], scl_[:, sl], dec_[:, sl], cdte_[:, sl]

    def ssd_dir(l, fwd, xtok, Txtok, dtt, Tdt, BTap, TBT, CTap, TCT, btk, Tbtk, Hst, THst, Hbf, THbf, emit_y, fillers=None):
        d0 = 0 if fwd else 16
        q, scl, dec, cdte = small_chain(l, dtt, Tdt, d0, strictU=not fwd)
        import os as _os
        st3 = _os.environ.get("KSTOP3", "") if not fwd else ""
        if st3 == "chain":
            return
        xdt, Txdt = tokb[4], T_tokb[4]
        xdte, Txdte = tokb[5], T_tokb[5]
        x3 = xtok.rearrange("p (a b) -> p a b", a=NH)
        c.tt("pool", [Txtok, Tdt], [Txdt], xdt.rearrange("p (a b) -> p a b", a=NH), x3, bc3(dtt[:, d0:d0 + 16], NH, HP), ALU.mult)
        c.tt("pool", [Txtok, T_sm[11]], [Txdte], xdte.rearrange("p (a b) -> p a b", a=NH), x3, bc3(cdte, NH, HP), ALU.mult)
        c.tt("pool", [T_sm[10]], [THst], Hst.rearrange("p (a b) -> p a b", a=NH), Hst.rearrange("p (a b) -> p a b", a=NH), bc3(dec, NH, HP), ALU.mult)
        c.mm(TBT + TCT, [T_pgt], [(pgt[:, g * 128:(g + 1) * 128], BTap(g), CTap(g), True, True) for g in range(NG)])
        c.tt("dve", [T_pgt, T_const], [T_Gm], Gm.rearrange("p (a b) -> p a b", a=NG), pgt.rearrange("p (a b) -> p a b", a=NG),
             (maskF if fwd else maskB).unsqueeze(1).to_broadcast([128, NG, 128]), ALU.mult)
        if st3 == "gm":
            return
        qring = [(pqbc, T_pqbc), (proj[0], T_proj[0]), (proj[1], T_proj[1])]
        for hg in range(NG):
            qb, Tqb = qring[hg % 3]
            c.mm([T_qTs, T_const], [Tqb], [(qb[:, i * 128:(i + 1) * 128], sel[0:16, (4 * hg + i) * 128:(4 * hg + i + 1) * 128], qTs[0:16, :], True, True) for i in range(4)])
            for i in range(4):
                h = 4 * hg + i
                c.ts("dve", [Tqb, T_sm[7]], [T_seg], seg[:, i * 128:(i + 1) * 128], qb[:, i * 128:(i + 1) * 128],
                     q[:, h:h + 1], ALU.subtract, 0.0, ALU.min if fwd else ALU.max)
            c.act([T_seg], [T_E], out=Eb, in_=seg, func=AF.Exp, scale=1.0 if fwd else -1.0)
            c.tt("dve", [T_E, T_Gm], [T_MT[hg]], MT[:, hg * 512:(hg + 1) * 512].rearrange("p (a b) -> p a b", a=4),
                 Eb.rearrange("p (a b) -> p a b", a=4), Gm[:, hg * 128:(hg + 1) * 128].unsqueeze(1).to_broadcast([128, 4, 128]), ALU.mult)
            if fillers is not None and fillers[hg] is not None:
                fillers[hg]()
        if st3 == "mt":
            return
        for half in range(2):
            c.mm([T_MT[2 * half], T_MT[2 * half + 1], Txdt], [T_ptok[0]],
                 [(ptok[0][:, j * 64:(j + 1) * 64], MT[:, (8 * half + j) * 128:(8 * half + j + 1) * 128],
                   xdt[:, (8 * half + j) * 64:(8 * half + j + 1) * 64], True, True) for j in range(8)])
            c.mm(TCT + [THbf], [T_ptok[1]],
                 [(ptok[1][:, j * 256:(j + 1) * 256], CTap(2 * half + j), Hbf[:, (2 * half + j) * 256:(2 * half + j + 1) * 256], True, True) for j in range(2)])
            emit_y(half, ptok[0], T_ptok[0], ptok[1], T_ptok[1], scl)
        if st3 == "y":
            return
        for half in range(2):
            c.mm([Tbtk, Txdte], [T_ptok[half]],
                 [(ptok[half][:, j * 256:(j + 1) * 256], btk[:, (2 * half + j) * 128:(2 * half + j + 1) * 128],
                   xdte[:, (2 * half + j) * 256:(2 * half + j + 1) * 256], True, True) for j in range(2)])
            c.tt("dve", [T_ptok[half]], [THst], Hst[:, half * 512:(half + 1) * 512], Hst[:, half * 512:(half + 1) * 512], ptok[half], ALU.add)
        c.copy("act", [THst], [THbf], Hbf, Hst)

    def S1(l, s):
        pvl, bvl = pv[l], bv[l]
        for t in T_xbc + T_ysc:
            c.inherit(t, T_a)
        c.dma("sp", wA[:, 0:KD * ZW], wbf[l, :, OFF_ZDT:OFF_ZDT + KD * ZW], [T_wseg[l][0]], [T_wA])
        c.dma("sp", wO, wbf[l, :, OFF_WO + 8 * 1024:OFF_WO + 16 * 1024], [T_wseg[l][1]], [T_wO])
        wz = wA[:, 0:KD * ZW].rearrange("p (a b) -> p a b", a=KD)
        wo3 = wO.rearrange("p (a b) -> p a b", a=8)
        Hst, THst, Hbf, THbf = tokf[2], T_tokf[2], tokb[6], T_tokb[6]
        c.op("pool", [], [THst], lambda e: e.memset(Hst, 0.0))
        c.op("pool", [], [THbf], lambda e: e.memset(Hbf, 0.0))
        pj = [0]
        for ti, (c0, nch) in enumerate(tiles):
            ncol = nch * 128
            W = ncol + 2
            nw, Tnw = nTw[ti % 2], T_nTw[ti % 2]
            rd = [T_nT[l][s][cc] for cc in range(max(c0 - 1, 0), min(c0 + nch + 1, NCH))] + [T_nTb[l][s]]
            c.dma("sp", nw[:, :, 0:W], nT_s[l][s][:, :, c0 * 128:c0 * 128 + W], rd, [Tnw])
            for fc in range(NFI):
                wt, Twt = ring_load(l, 2, OFF_WI + fc * 1024)
                pb, Tpb = proj[pj[0] % 2], T_proj[pj[0] % 2]
                pj[0] += 1
                c.mm([Twt, Tnw], [Tpb], [(pb[:, 0:W], wt[:, k, :], nw[:, k, 0:W], k == 0, k == KD - 1) for k in range(KD)])
                if fc < 16:
                    w0 = pvl[:, PV_CWX + fc * 3 + 0:PV_CWX + fc * 3 + 1]
                    w1 = pvl[:, PV_CWX + fc * 3 + 1:PV_CWX + fc * 3 + 2]
                    w2 = pvl[:, PV_CWX + fc * 3 + 2:PV_CWX + fc * 3 + 3]
                    bb = pvl[:, PV_CBX + fc:PV_CBX + fc + 1]
                    acc, Tacc = (accA, T_accA) if fc % 2 == 0 else (accB, T_accB)
                    c.act([Tpb, T_pv], [Tacc], out=acc[:, 0:ncol], in_=pb[:, 1:1 + ncol], func=AF.Identity, scale=w1, bias=bb)
                    c.stt([Tpb, T_pv, Tacc], [Tacc], acc[:, 0:ncol], pb[:, 0:ncol], w0, acc[:, 0:ncol], ALU.mult, ALU.add)
                    c.stt([Tpb, T_pv, Tacc], [Tacc], acc[:, 0:ncol], pb[:, 2:2 + ncol], w2, acc[:, 0:ncol], ALU.mult, ALU.add)
                    c.act([Tacc], [T_xbc[fc]], out=xbc[:, fc, 0:ncol], in_=acc[:, 0:ncol], func=AF.Silu)
                else:
                    j, r = divmod(fc - 16, 3)
                    if r == 0:
                        c.copy("act", [Tpb], [T_cbuf], cbuf[:, 0:W], pb[:, 0:W])
                    elif r == 1:
                        w0 = pvl[:, PV_CWS + j * 3 + 0:PV_CWS + j * 3 + 1]
                        w1 = pvl[:, PV_CWS + j * 3 + 1:PV_CWS + j * 3 + 2]
                        w2 = pvl[:, PV_CWS + j * 3 + 2:PV_CWS + j * 3 + 3]
                        c.tt("dve", [Tpb, T_cbuf], [T_cv], cvb[:, 0:W], pb[:, 0:W], cbuf[:, 0:W], ALU.mult)
                        c.act([T_cv, T_pv], [T_sg], out=sgb[:, 0:ncol], in_=cvb[:, 1:1 + ncol], func=AF.Copy, scale=w1)
                        c.stt([T_cv, T_pv, T_sg], [T_sg], sgb[:, 0:ncol], cvb[:, 0:ncol], w0, sgb[:, 0:ncol], ALU.mult, ALU.add)
                        c.stt([T_cv, T_pv, T_sg], [T_sg], sgb[:, 0:ncol], cvb[:, 2:2 + ncol], w2, sgb[:, 0:ncol], ALU.mult, ALU.add)
                    else:
                        c.tt("dve", [Tpb, T_sg], [T_ysc[j]], ysc[:, j, 0:ncol], pb[:, 1:1 + ncol], sgb[:, 0:ncol], ALU.mult)
            def zdt_mm(jc, part):
                cidx = c0 + jc
                ns0 = 1 + jc * 128
                szb, Tsz = tokb[cidx % 2], T_tokb[cidx % 2]
                dtt, Tdt = dts[cidx % 2], T_dts[cidx % 2]
                if part == 0:
                    for hh in range(2):
                        c.mm([Tnw, T_wA], [T_ptok[hh]],
                             [(ptok[hh], nw[:, k, ns0:ns0 + 128], wz[:, k, hh * 512:(hh + 1) * 512], k == 0, k == KD - 1) for k in range(KD)])
                        c.act([T_ptok[hh]], [Tsz], out=szb[:, hh * 512:(hh + 1) * 512], in_=ptok[hh], func=AF.Silu)
                    c.dma("sp", sz_s[l][s][cidx], szb, [Tsz], [T_scr["sz"][l][s][cidx]])
                else:
                    c.mm([Tnw, T_wA], [T_psm[0]],
                         [(psm[:, 0:32], nw[:, k, ns0:ns0 + 128], wz[:, k, 1024:1056], k == 0, k == KD - 1) for k in range(KD)])
                    xb_, m_, e_, l_ = (sm[:, i, :] for i in range(0, 4))
                    c.tt("dve", [T_psm[0], T_pv], [T_sm[0]], xb_, psm[:, 0:32], bvl[:, BV_DTB:BV_DTB + 32], ALU.add)
                    c.ts("dve", [T_sm[0]], [T_sm[1]], m_, xb_, 30.0, ALU.min)
                    c.act([T_sm[1]], [T_sm[2]], out=e_, in_=m_, func=AF.Exp)
                    c.act([T_sm[2]], [T_sm[3]], out=l_, in_=e_, func=AF.Ln, bias=1.0)
                    c.tt("dve", [T_sm[3], T_sm[0]], [Tdt], dtt, l_, xb_, ALU.max)
                    c.dma("sp", dt_s[l][s][cidx], dtt, [Tdt], [T_scr["dt"][l][s][cidx]])

            def sc_out(jc, half, hb, Thb, hpb, Thp):
                cs0 = jc * 128
                c.mm(T_ysc + [T_wO], [T_ptok[half]],
                     [(ptok[half], ysc[:, k, cs0:cs0 + 128], wo3[:, k, half * 512:(half + 1) * 512], k == 0, k == 7) for k in range(8)])
                c.tt("dve", [T_ptok[half], Thb], [Thp], hpb[:, half * 512:(half + 1) * 512], hb[:, half * 512:(half + 1) * 512], ptok[half], ALU.add)
                if half == 1:
                    c.dma("sp", hp_s[l][s][c0 + jc], hpb, [Thp], [T_scr["hp"][l][s][c0 + jc]])

            zdt_mm(0, 0)
            zdt_mm(0, 1)
            for jc in range(nch):
                cidx = c0 + jc
                cs0 = jc * 128
                dtt, Tdt = dts[cidx % 2], T_dts[cidx % 2]
                hb, Thb = tokf[cidx % 2], T_tokf[cidx % 2]
                if l == 0:
                    c.dma("sp", hb, x_d[s, cidx * 128:(cidx + 1) * 128, :], [T_x], [Thb])
                else:
                    c.dma("sp", hb, ho_s[l - 1][s][cidx], [T_scr["ho"][l - 1][s][cidx]], [Thb])
                hpb, Thp = tokf[4], T_tokf[4]
                c.tr([T_xbc[k] for k in range(8)] + [T_idb], [T_pxtr],
                     [(pxtr[:, k * 128:(k + 1) * 128], xbc[:, k, cs0:cs0 + 128]) for k in range(8)], idb)
                xtok, Txtok = tokb[2], T_tokb[2]
                c.copy("act", [T_pxtr], [Txtok], xtok, pxtr)
                c.dma("sp", xt_s[l][s][cidx], xtok, [Txtok], [T_scr["xt"][l][s][cidx]])
                c.tr([T_xbc[8 + g] for g in range(4)] + [T_idb], [T_pbtr],
                     [(pbtr[:, g * 128:(g + 1) * 128], xbc[:, 8 + g, cs0:cs0 + 128]) for g in range(4)], idb)
                btk, Tbtk = btok[0], T_btok[0]
                c.copy("dve", [T_pbtr], [Tbtk], btk, pbtr)
                c.dma("sp", bt_s[l][s][cidx], btk, [Tbtk], [T_scr["bt"][l][s][cidx]])
                TB4 = [T_xbc[8 + g] for g in range(4)]
                TC4 = [T_xbc[12 + g] for g in range(4)]
                c.dma("sp", BT_s[l][s][cidx].rearrange("p (a b) -> p a b", a=4), xbc[:, 8:12, cs0:cs0 + 128], TB4, [T_scr["BT"][l][s][cidx]])
                c.dma("sp", CT_s[l][s][cidx].rearrange("p (a b) -> p a b", a=4), xbc[:, 12:16, cs0:cs0 + 128], TC4, [T_scr["CT"][l][s][cidx]])
                ypb, Typ = tokf[3], T_tokf[3]
                t1, Tt1 = tokf[5], T_tokf[5]

                def emit_y(half, dg, Tdg, of, Tof, scl, xtok=xtok, Txtok=Txtok, ypb=ypb, Typ=Typ, t1=t1, Tt1=Tt1):
                    hs = slice(half * 512, (half + 1) * 512)
                    t1h = t1[:, hs].rearrange("p (a b) -> p a b", a=8)
                    c.tt("dve", [Tof, T_sm[9]], [Tt1], t1h, of.rearrange("p (a b) -> p a b", a=8), bc3(scl[:, 8 * half:8 * half + 8], 8, HP), ALU.mult)
                    c.tt("dve", [Tdg, Tt1], [Tt1], t1[:, hs], dg, t1[:, hs], ALU.add)
                    yph = ypb[:, hs].rearrange("p (a b) -> p a b", a=8)
                    c.tt("pool", [Txtok, T_pv], [Typ], yph, xtok[:, hs].rearrange("p (a b) -> p a b", a=8),
                         bc3(bvl[:, BV_D + 8 * half:BV_D + 8 * half + 8], 8, HP), ALU.mult)
                    c.tt("dve", [Tt1, Typ], [Typ], ypb[:, hs], ypb[:, hs], t1[:, hs], ALU.add)

                nxt = jc + 1 < nch
                fillers = [
                    (lambda jc=jc, hb=hb, Thb=Thb, hpb=hpb, Thp=Thp: sc_out(jc, 0, hb, Thb, hpb, Thp)),
                    (lambda jc=jc, hb=hb, Thb=Thb, hpb=hpb, Thp=Thp: sc_out(jc, 1, hb, Thb, hpb, Thp)),
                    (lambda jc=jc: zdt_mm(jc + 1, 0)) if nxt else None,
                    (lambda jc=jc: zdt_mm(jc + 1, 1)) if nxt else None,
                ]
                ssd_dir(l, True, xtok, Txtok, dtt, Tdt,
                        lambda g: xbc[:, 8 + g, cs0:cs0 + 128], TB4, lambda g: xbc[:, 12 + g, cs0:cs0 + 128], TC4,
                        btk, Tbtk, Hst, THst, Hbf, THbf, emit_y, fillers=fillers)
                c.dma("sp", yp_s[l][s][cidx], ypb, [Typ], [T_scr["yp"][l][s][cidx]])

    def S2(l, s, last):
        pvl, bvl = pv[l], bv[l]
        c.dma("sp", wO, wbf[l, :, OFF_WO:OFF_WO + 8 * 1024], [T_wseg[l][1]], [T_wO])
        wo3 = wO.rearrange("p (a b) -> p a b", a=8)
        Hst, THst, Hbf, THbf = tokf[2], T_tokf[2], tokb[6], T_tokb[6]
        c.op("pool", [], [THst], lambda e: e.memset(Hst, 0.0))
        c.op("pool", [], [THbf], lambda e: e.memset(Hbf, 0.0))
        normw = bvl[:, BV_NW:BV_NW + 1024]
        pending = None
        for it, cidx in enumerate(reversed(range(NCH))):
            p2 = it % 2
            ypb, Typ = tokf[p2], T_tokf[p2]
            hpb, Thp = tokf[3 + p2], T_tokf[3 + p2]
            szb, Tsz = tokb[p2], T_tokb[p2]
            xtok, Txtok = tokb[2 + p2], T_tokb[2 + p2]
            btk, Tbtk = btok[p2], T_btok[p2]
            BTt, TBTt = BTb[p2], T_BTb[p2]
            CTt, TCTt = CTb[p2], T_CTb[p2]
            dtt, Tdt = dts[p2], T_dts[p2]
            TS = lambda nm: T_scr[nm][l][s][cidx]
            c.dma("sp", dtt, dt_s[l][s][cidx], [TS("dt")], [Tdt])
            c.dma("sp", xtok, xt_s[l][s][cidx], [TS("xt")], [Txtok])
            c.dma("sp", BTt, BT_s[l][s][cidx], [TS("BT")], [TBTt])
            c.dma("sp", CTt, CT_s[l][s][cidx], [TS("CT")], [TCTt])
            c.dma("sp", btk, bt_s[l][s][cidx], [TS("bt")], [Tbtk])
            c.dma("sp", ypb, yp_s[l][s][cidx], [TS("yp")], [Typ])
            c.dma("sp", szb, sz_s[l][s][cidx], [TS("sz")], [Tsz])
            c.dma("sp", hpb, hp_s[l][s][cidx], [TS("hp")], [Thp])
            t1, Tt1 = tokf[5], T_tokf[5]
            import os as _os
            st2 = _os.environ.get("KSTOP2", "")
            if st2 == "loads":
                continue

            def emit_y(half, dg, Tdg, of, Tof, scl, ypb=ypb, Typ=Typ, szb=szb, Tsz=Tsz, t1=t1, Tt1=Tt1):
                hs = slice(half * 512, (half + 1) * 512)
                t1h = t1[:, hs].rearrange("p (a b) -> p a b", a=8)
                c.tt("dve", [Tof, T_sm[9]], [Tt1], t1h, of.rearrange("p (a b) -> p a b", a=8), bc3(scl[:, 8 * half:8 * half + 8], 8, HP), ALU.mult)
                c.tt("dve", [Tdg, Tt1], [Tt1], t1[:, hs], dg, t1[:, hs], ALU.add)
                c.tt("dve", [Tt1, Typ], [Tt1], t1[:, hs], t1[:, hs], ypb[:, hs], ALU.add)
                c.tt("dve", [Tt1, Tsz], [Tt1], t1[:, hs], t1[:, hs], szb[:, hs], ALU.mult)

            ssd_dir(l, False, xtok, Txtok, dtt, Tdt,
                    lambda g: BTt[:, g * 128:(g + 1) * 128], [TBTt], lambda g: CTt[:, g * 128:(g + 1) * 128], [TCTt],
                    btk, Tbtk, Hst, THst, Hbf, THbf, emit_y, fillers=pending)

            def make_post(cidx=cidx, hpb=hpb, Thp=Thp, nbb=szb, Tnb=Tsz, t1=t1, Tt1=Tt1):
                junk, Tjunk = tokb[7], T_tokb[7]
                ynb, Tyn = tokb[7], T_tokb[7]

                def p0():
                    for g in range(NG):
                        c.act([Tt1], [Tjunk, T_ss], out=junk[:, g * 256:(g + 1) * 256], in_=t1[:, g * 256:(g + 1) * 256], func=AF.Square, accum_out=ss[:, g:g + 1])
                    rstd_from_ss(NG, 1.0 / 256)
                    for g in range(NG):
                        c.stt([Tt1, T_rs, T_pv], [Tyn], ynb[:, g * 256:(g + 1) * 256], t1[:, g * 256:(g + 1) * 256], rs[:, g:g + 1],
                              normw[:, g * 256:(g + 1) * 256], ALU.mult, ALU.mult)

                def outp(half):
                    c.mm([T_ynT, T_wO], [T_ptok[half]],
                         [(ptok[half], ynT[:, k, :], wo3[:, k, half * 512:(half + 1) * 512], k == 0, k == 7) for k in range(8)])
                    c.tt("dve", [T_ptok[half], Thp], [Thp], hpb[:, half * 512:(half + 1) * 512], hpb[:, half * 512:(half + 1) * 512], ptok[half], ALU.add)

                def p1():
                    c.tr([Tyn, T_idb], [T_pxtr], [(pxtr[:, k * 128:(k + 1) * 128], ynb[:, k * 128:(k + 1) * 128]) for k in range(KD)], idb)
                    c.copy("act", [T_pxtr], [T_ynT], ynT, pxtr.rearrange("p (a b) -> p a b", a=KD))
                    outp(0)

                def p2():
                    outp(1)
                    c.dma("sp", hm_s[l][s][cidx], hpb, [Thp], [T_scr["hm"][l][s][cidx]])

                def p3():
                    norm_to_nT(l, hpb, Thp, PV_N2, n2T_s[l][s], T_n2T[l][s][cidx], cidx, tokb[7], T_tokb[7], nbb, Tnb)

                return [p0, p1, p2, p3]

            pending = make_post()
        for f in pending:
            f()

    def S3(l, s, last, bg=None):
        pvl = pv[l]
        for t in T_a:
            c.inherit(t, T_xbc + T_ysc)
        c.dma("sp", wA, wbf[l, :, OFF_WD:OFF_WD + KF * 1024], [T_wseg[l][4]], [T_wA])
        wd3 = wA.rearrange("p (a b) -> p a b", a=KF)
        pj = [0]
        pring3 = [(proj[0], T_proj[0]), (proj[1], T_proj[1]), (pgt, T_pgt), (pqbc, T_pqbc)]
        for ti, (c0, nch) in enumerate(tiles):
            ncol = nch * 128
            W = ncol + 2
            nw, Tnw = nTw[ti % 2], T_nTw[ti % 2]
            rd = [T_n2T[l][s][cc] for cc in range(max(c0 - 1, 0), min(c0 + nch + 1, NCH))] + [T_n2Tb[l][s]]
            c.dma("sp", nw[:, :, 0:W], n2T_s[l][s][:, :, c0 * 128:c0 * 128 + W], rd, [Tnw])
            for j in range(KF):
                accs = []
                for r in range(2):
                    fc = 2 * j + r
                    wt, Twt = ring_load(l, 3, OFF_WU + fc * 1024)
                    pb, Tpb = pring3[pj[0] % 4]
                    pj[0] += 1
                    c.mm([Twt, Tnw], [Tpb], [(pb[:, 0:W], wt[:, k, :], nw[:, k, 0:W], k == 0, k == KD - 1) for k in range(KD)])
                    w0 = pvl[:, PV_CWF + fc * 3 + 0:PV_CWF + fc * 3 + 1]
                    w1 = pvl[:, PV_CWF + fc * 3 + 1:PV_CWF + fc * 3 + 2]
                    w2 = pvl[:, PV_CWF + fc * 3 + 2:PV_CWF + fc * 3 + 3]
                    bb = pvl[:, PV_CBF + fc:PV_CBF + fc + 1]
                    if j % 2 == 0:
                        acc, Tacc = (accA, T_accA) if r == 0 else (accB, T_accB)
                        sg_, Tsg_ = sgb, T_sg
                    else:
                        acc, Tacc = (accC, T_accC) if r == 0 else (accD, T_accD)
                        sg_, Tsg_ = sgb2, T_sg2
                    c.act([Tpb, T_pv], [Tacc], out=acc[:, 0:ncol], in_=pb[:, 1:1 + ncol], func=AF.Identity, scale=w1, bias=bb)
                    c.stt([Tpb, T_pv, Tacc], [Tacc], acc[:, 0:ncol], pb[:, 0:ncol], w0, acc[:, 0:ncol], ALU.mult, ALU.add)
                    c.stt([Tpb, T_pv, Tacc], [Tacc], acc[:, 0:ncol], pb[:, 2:2 + ncol], w2, acc[:, 0:ncol], ALU.mult, ALU.add)
                    if r == 0:
                        c.act([Tacc], [Tsg_], out=sg_[:, 0:ncol], in_=acc[:, 0:ncol], func=AF.Silu)
                c.tt("pool", [Tsg_, Tacc], [T_a[j]], abuf[:, j, 0:ncol], sg_[:, 0:ncol], acc[:, 0:ncol], ALU.mult)
            for jc in range(nch):
                cidx = c0 + jc
                cs0 = jc * 128
                hb, Thb = tokf[cidx % 2], T_tokf[cidx % 2]
                c.dma("sp", hb, hm_s[l][s][cidx], [T_scr["hm"][l][s][cidx]], [Thb])
                for half in range(2):
                    c.mm(T_a + [T_wA], [T_ptok[half]],
                         [(ptok[half], abuf[:, k, cs0:cs0 + 128], wd3[:, k, half * 512:(half + 1) * 512], k == 0, k == KF - 1) for k in range(KF)])
                    c.tt("dve", [T_ptok[half], Thb], [Thb], hb[:, half * 512:(half + 1) * 512], hb[:, half * 512:(half + 1) * 512], ptok[half], ALU.add)
                if not last:
                    c.dma("sp", ho_s[l][s][cidx], hb, [Thb], [T_scr["ho"][l][s][cidx]])
                else:
                    junk, Tjunk = tokb[7], T_tokb[7]
                    ob, Tob = tokf[4 + cidx % 2], T_tokf[4 + cidx % 2]
                    c.act([Thb], [Tjunk, T_ss], out=junk, in_=hb, func=AF.Square, accum_out=ss[:, 0:1])
                    rstd_from_ss(1, 1.0 / D)
                    c.stt([Thb, T_rs, T_pv], [Tob], ob, hb, rs[:, 0:1], fnw, ALU.mult, ALU.mult)
                    c.dma("sp", out_d[s, cidx * 128:(cidx + 1) * 128, :], ob, [Tob], [T_out[s][cidx]])
            if bg is not None:
                for _ in range(TCH):
                    next(bg, None)

    units = [(l, s) for l in range(DEPTH) for s in range(NSEQ)]
    pend = P1(*units[0])
    for ui, (l, s) in enumerate(units):
        for _ in pend:
            pass
        S1(l, s)
        S2(l, s, l == DEPTH - 1)
        pend = P1(*units[ui + 1]) if ui + 1 < len(units) else iter(())
        S3(l, s, l == DEPTH - 1, bg=pend)
    for s in range(NSEQ):
        for cc in range(NCH):
            for ev in T_out[s][cc].w.values():
                c._wait("sp", ev)
    if dbg:
        for nm in T_scr:
            for l in range(DEPTH):
                for s in range(NSEQ):
                    for cc in range(NCH):
                        for ev in T_scr[nm][l][s][cc].w.values():
                            c._wait("sp", ev)
    build.stats = (c.nins, c.nwaits)
    return nc


def prep_params(inp, DEPTH):
    f = lambda a: np.ascontiguousarray(np.asarray(a, dtype=np.float32))
    w_in = f(inp["w_in"])

    def pk(w):
        d, r, cc = w.shape
        return w.reshape(d, r // 128, 128, cc).transpose(0, 2, 1, 3)

    cols = [1024 + j * 128 for j in range(8)] + [2048 + j * 128 for j in range(4)] + [2560 + j * 128 for j in range(4)]
    for j in range(8):
        cols += [4128 + j * 128, 5152 + j * 128, 3104 + j * 128]
    wi = np.stack([pk(w_in[:, :, c0:c0 + 128]) for c0 in cols], axis=2).reshape(DEPTH, 128, NFI * 1024)
    wzdt = pk(np.concatenate([w_in[:, :, 0:1024], w_in[:, :, 3072:3104]], axis=2)).reshape(DEPTH, 128, KD * ZW)
    wo = pk(f(inp["w_out"])).reshape(DEPTH, 128, 16 * 1024)
    w_up = f(inp["w_ffn_up"])
    ucols = []
    for j in range(KF):
        ucols += [j * 128, 2816 + j * 128]
    wu = np.stack([pk(w_up[:, :, c0:c0 + 128]) for c0 in ucols], axis=2).reshape(DEPTH, 128, NFU * 1024)
    wd = pk(f(inp["w_ffn_down"])).reshape(DEPTH, 128, KF * 1024)
    wall = np.ascontiguousarray(np.concatenate([wzdt, wo, wi, wu, wd], axis=2))
    assert wall.shape == (DEPTH, 128, XT)

    pvec = np.zeros((DEPTH, 128, NPV), np.float32)
    uidx = np.concatenate([np.arange(c0, c0 + 128) for c0 in ucols])
    for l in range(DEPTH):
        pvec[l, :, PV_CWX:PV_CWX + 48] = f(inp["conv_xbc_w"])[l].reshape(3, 16, 128).transpose(2, 1, 0).reshape(128, 48)
        pvec[l, :, PV_CBX:PV_CBX + 16] = f(inp["conv_xbc_b"])[l].reshape(16, 128).T
        pvec[l, :, PV_CWS:PV_CWS + 24] = f(inp["sc_conv_w"])[l].reshape(3, 8, 128).transpose(2, 1, 0).reshape(128, 24)
        pvec[l, :, PV_CWF:PV_CWF + 132] = f(inp["ffn_conv_w"])[l][:, uidx].reshape(3, NFU, 128).transpose(2, 1, 0).reshape(128, 132)
        pvec[l, :, PV_CBF:PV_CBF + 44] = f(inp["ffn_conv_b"])[l][uidx].reshape(NFU, 128).T
        pvec[l, :, PV_N1:PV_N1 + 8] = f(inp["norm1_w"])[l].reshape(8, 128).T
        pvec[l, :, PV_N2:PV_N2 + 8] = f(inp["norm2_w"])[l].reshape(8, 128).T
    bvec = np.zeros((DEPTH, NBV), np.float32)
    for l in range(DEPTH):
        bvec[l, BV_NW:BV_NW + 1024] = f(inp["ssd_norm_w"])[l]
        bvec[l, BV_D:BV_D + 16] = f(inp["d_skip"])[l]
        bvec[l, BV_DTB:BV_DTB + 16] = f(inp["dt_bias_f"])[l]
        bvec[l, BV_DTB + 16:BV_DTB + 32] = f(inp["dt_bias_b"])[l]
        bvec[l, BV_ALOG:BV_ALOG + 16] = f(inp["a_log_f"])[l]
        bvec[l, BV_ALOG + 16:BV_ALOG + 32] = f(inp["a_log_b"])[l]
    fnw = f(inp["final_norm_w"]).reshape(1, D)
    t = np.arange(128)
    consts = np.zeros((128, 768), np.float32)
    consts[:, 0:128] = np.eye(128)
    consts[:, 128:256] = (t[:, None] <= t[None, :])
    consts[:, 256:384] = (t[:, None] < t[None, :])
    consts[:, 384:512] = (t[None, :] >= t[:, None])
    consts[:, 512:640] = (t[None, :] <= t[:, None])
    consts[127, 640:768] = 1.0
    sel = np.zeros((128, 2048), np.float32)
    for h in range(16):
        sel[h, h * 128:(h + 1) * 128] = 1.0
    return dict(wall=wall, pvec=pvec, bvec=bvec, fnw=fnw, consts=consts, sel=sel)


_NC_CACHE = {}


def kernel(**inputs):
    x = np.ascontiguousarray(np.asarray(inputs["x"], dtype=np.float32))
    B, L, _ = x.shape
    DEPTH = np.asarray(inputs["w_in"]).shape[0]
    ncores = 8
    NSEQ = B // ncores
    params = prep_params(inputs, DEPTH)
    key = (L, NSEQ, DEPTH)
    if key not in _NC_CACHE:
        _NC_CACHE[key] = build(L, NSEQ, DEPTH)
    nc = _NC_CACHE[key]
    in_maps = []
    for i in range(ncores):
        m = dict(params)
        m["x"] = np.ascontiguousarray(x[i * NSEQ:(i + 1) * NSEQ])
        in_maps.append(m)
    res = run_bass_kernel_spmd(nc, in_maps, core_ids=list(range(ncores)))
    out = np.concatenate([np.asarray(r["out"]) for r in res.results], axis=0)
    return out.astype(np.float32, copy=False)
```

```python
import numpy as np
import concourse.bass as bass
import concourse.mybir as mybir
from concourse.bass_utils import run_bass_kernel_spmd

F32 = mybir.dt.float32
BF16 = mybir.dt.bfloat16
ALU = mybir.AluOpType
AF = mybir.ActivationFunctionType

D = 1024
KD = 8
NH = 16
HP = 64
NG = 4
KF = 22
NFI = 40
NFU = 44
EPS = 1e-5
ZW = 1056
OFF_ZDT = 0
OFF_WO = OFF_ZDT + KD * ZW
OFF_WI = OFF_WO + 16 * 1024
OFF_WU = OFF_WI + NFI * 1024
OFF_WD = OFF_WU + NFU * 1024
XT = OFF_WD + KF * 1024
PV_CWX, PV_CBX, PV_CWS, PV_CWF, PV_CBF, PV_N1, PV_N2, NPV = 0, 48, 64, 88, 220, 264, 272, 280
BV_NW, BV_D, BV_DTB, BV_ALOG, NBV = 0, 1024, 1040, 1072, 1104
TCH = 3


class T:
    __slots__ = ("name", "w", "r")

    def __init__(self, name):
        self.name = name
        self.w = {}
        self.r = {}


class Ctx:
    CE = ("pe", "act", "dve", "pool")

    def __init__(self, nc, ndma_sems=16):
        self.nc = nc
        self.eng = {"pe": nc.tensor, "act": nc.scalar, "dve": nc.vector, "pool": nc.gpsimd, "sp": nc.sync}
        self.sem = {e: nc.alloc_semaphore("s_" + e) for e in self.CE}
        self.cnt = {e: 0 for e in self.CE}
        self.waited = {e: {} for e in self.eng}
        self.dsem, self.dcnt, self.drr = {}, {}, {}
        for q in ("sp", "pool"):
            self.dsem[q] = [nc.alloc_semaphore("d_%s%d" % (q, i)) for i in range(ndma_sems)]
            self.dcnt[q] = [0] * ndma_sems
            self.drr[q] = 0
        self.nwaits = 0
        self.nins = 0

    def _wait(self, e, ev):
        sem, val = ev
        key = id(sem)
        if self.waited[e].get(key, 0) >= val:
            return
        self.waited[e][key] = val
        self.eng[e].wait_ge(sem, val)
        self.nwaits += 1

    def _deps(self, e, reads, writes, part=False):
        mysem = self.sem.get(e)
        for t in reads:
            for ev in t.w.values():
                self._wait(e, ev)
        for t in writes:
            if not part:
                for ev in t.w.values():
                    self._wait(e, ev)
            for ev in t.r.values():
                self._wait(e, ev)

    def _mark(self, ev, reads, writes, part=False):
        k = id(ev[0])
        for t in reads:
            t.r[k] = ev
        for t in writes:
            if part:
                t.w[k] = ev
            else:
                t.w = {k: ev}
            t.r = {}

    def op(self, e, reads, writes, emit):
        self._deps(e, reads, writes)
        ins = emit(self.eng[e])
        self.cnt[e] += 1
        ins.then_inc(self.sem[e], 1)
        self._mark((self.sem[e], self.cnt[e]), reads, writes)
        self.nins += 1
        return ins

    def dma(self, q, out, in_, reads, writes, part=False, slow=False):
        self._deps(q, reads, writes, part)
        k = self.drr[q]
        self.drr[q] = (k + 1) % len(self.dsem[q])
        sem = self.dsem[q][k]
        if self.dcnt[q][k] > 0:
            self._wait(q, (sem, self.dcnt[q][k]))
        self.dcnt[q][k] += 16
        if slow:
            self.eng[q].dma_start(out=out, in_=in_, allow_slow_non_contiguous=True).then_inc(sem, 16)
        else:
            self.eng[q].dma_start(out=out, in_=in_).then_inc(sem, 16)
        self._mark((sem, self.dcnt[q][k]), reads, writes, part)
        self.nins += 1

    def inherit(self, new, olds):
        for o in olds:
            for k, ev in list(o.w.items()) + list(o.r.items()):
                if k not in new.r or new.r[k][1] < ev[1]:
                    new.r[k] = ev

    def act(self, reads, writes, **kw):
        return self.op("act", reads, writes, lambda e: e.activation(**kw))

    def tt(self, eng, reads, writes, out, in0, in1, op):
        return self.op(eng, reads, writes, lambda e: e.tensor_tensor(out=out, in0=in0, in1=in1, op=op))

    def ts(self, eng, reads, writes, out, in0, s1, op0, s2=None, op1=None):
        if op1 is None:
            return self.op(eng, reads, writes, lambda e: e.tensor_single_scalar(out=out, in_=in0, scalar=s1, op=op0))
        return self.op(eng, reads, writes, lambda e: e.tensor_scalar(out=out, in0=in0, scalar1=s1, scalar2=s2, op0=op0, op1=op1))

    def stt(self, reads, writes, out, in0, scalar, in1, op0, op1):
        return self.op("dve", reads, writes, lambda e: e.scalar_tensor_tensor(out=out, in0=in0, scalar=scalar, in1=in1, op0=op0, op1=op1))

    def copy(self, eng, reads, writes, out, in_):
        if eng == "act":
            return self.act(reads, writes, out=out, in_=in_, func=AF.Copy)
        return self.op(eng, reads, writes, lambda e: e.tensor_copy(out=out, in_=in_))

    def mm(self, reads, writes, items):
        def emit(e):
            ins = None
            for (o, l, r, st, sp) in items:
                ins = e.matmul(o, lhsT=l, rhs=r, start=st, stop=sp)
            return ins
        return self.op("pe", reads, writes, emit)

    def tr(self, reads, writes, items, ident):
        def emit(e):
            ins = None
            for (o, i) in items:
                ins = e.transpose(out=o, in_=i, identity=ident)
            return ins
        return self.op("pe", reads, writes, emit)


def bc3(ap, n_outer, n_inner):
    return ap.unsqueeze(2).to_broadcast([128, n_outer, n_inner])


def build(L, NSEQ, DEPTH, dbg=False):
    NCH = L // 128
    tiles = [(c0, min(TCH, NCH - c0)) for c0 in range(0, NCH, TCH)]
    nc = bass.Bass("TRN2", target_bir_lowering=False)
    c = Ctx(nc)
    skind = "ExternalOutput" if dbg else "Internal"

    def din(name, shape):
        return nc.dram_tensor(name, list(shape), F32, kind="ExternalInput").ap()

    def dscr(name, shape, dt):
        return nc.dram_tensor(name, list(shape), dt, kind=skind).ap()

    x_d = din("x", [NSEQ, L, D])
    wall_d = din("wall", [DEPTH, 128, XT])
    pvec_d = din("pvec", [DEPTH, 128, NPV])
    bvec_t = nc.dram_tensor("bvec", [DEPTH, NBV], F32, kind="ExternalInput")
    fnw_t = nc.dram_tensor("fnw", [1, D], F32, kind="ExternalInput")
    consts_d = din("consts", [128, 768])
    sel_d = din("sel", [128, 2048])
    out_d = nc.dram_tensor("out", [NSEQ, L, D], F32, kind="ExternalOutput").ap()

    wbf = dscr("wbf", [DEPTH, 128, XT], BF16)
    nT_s = [[dscr("nT_%d_%d" % (l, s), [128, KD, L + 2], BF16) for s in range(NSEQ)] for l in range(DEPTH)]
    n2T_s = [[dscr("n2T_%d_%d" % (l, s), [128, KD, L + 2], BF16) for s in range(NSEQ)] for l in range(DEPTH)]

    def per_chunk(name, shape, dt):
        return [[dscr("%s_%d_%d" % (name, l, s), [NCH] + list(shape), dt) for s in range(NSEQ)] for l in range(DEPTH)]

    yp_s = per_chunk("yp", [128, D], F32)
    xt_s = per_chunk("xt", [128, D], BF16)
    bt_s = per_chunk("bt", [128, 512], BF16)
    BT_s = per_chunk("BT", [128, 512], BF16)
    CT_s = per_chunk("CT", [128, 512], BF16)
    sz_s = per_chunk("sz", [128, D], BF16)
    dt_s = per_chunk("dt", [128, 32], F32)
    hp_s = per_chunk("hp", [128, D], F32)
    hm_s = per_chunk("hm", [128, D], F32)
    ho_s = [[dscr("ho_%d_%d" % (l, s), [NCH, 128, D], F32) for s in range(NSEQ)] for l in range(DEPTH - 1)]

    def mkT(name, *dims):
        if not dims:
            return T(name)
        return [mkT("%s_%d" % (name, i), *dims[1:]) for i in range(dims[0])]

    T_x = T("x")
    T_wseg = mkT("wseg", DEPTH, 5)
    T_nT = mkT("nT", DEPTH, NSEQ, NCH)
    T_nTb = mkT("nTb", DEPTH, NSEQ)
    T_n2T = mkT("n2T", DEPTH, NSEQ, NCH)
    T_n2Tb = mkT("n2Tb", DEPTH, NSEQ)
    T_scr = {nm: mkT(nm, DEPTH, NSEQ, NCH) for nm in ("yp", "xt", "bt", "BT", "CT", "sz", "dt", "hp", "hm", "ho")}
    T_out = mkT("out", NSEQ, NCH)

    def sb(name, shape, dt=F32):
        return nc.alloc_sbuf_tensor("s_" + name, list(shape), dt).ap()

    consts = sb("consts", [128, 768]); T_const = T("consts")
    sel = sb("sel", [128, 2048])
    idb = sb("idb", [128, 128], BF16); T_idb = T("idb")
    ident_f = consts[:, 0:128]
    Uincl = consts[:, 128:256]
    Ustrict = consts[:, 256:384]
    maskF = consts[:, 384:512]
    maskB = consts[:, 512:640]
    Sel127 = consts[:, 640:768]
    pv = [sb("pv%d" % l, [128, NPV]) for l in range(DEPTH)]
    bv = [sb("bv%d" % l, [128, NBV]) for l in range(DEPTH)]
    T_pv = T("pv")
    Abc = [sb("Abc%d" % l, [128, 32]) for l in range(DEPTH)]; T_A = T("Abc")
    fnw = sb("fnw", [128, D])
    zcol = sb("zcol", [128, KD, 1], BF16); T_zcol = T("zcol")

    wA = sb("wA", [128, KF * 1024], BF16); T_wA = T("wA")
    wO = sb("wO", [128, 8 * 1024], BF16); T_wO = T("wO")
    NRING = 6
    ring = [sb("ring%d" % i, [128, KD, 128], BF16) for i in range(NRING)]; T_ring = mkT("ring", NRING)
    WMAX = TCH * 128 + 2
    nTw = [sb("nTw%d" % i, [128, KD, WMAX], BF16) for i in range(2)]; T_nTw = mkT("nTw", 2)
    big = sb("big", [128, 24 * 384], BF16)
    xbc = big[:, 0:16 * 384].rearrange("p (a b) -> p a b", a=16)
    ysc = big[:, 16 * 384:24 * 384].rearrange("p (a b) -> p a b", a=8)
    abuf = big[:, 0:KF * 384].rearrange("p (a b) -> p a b", a=KF)
    T_xbc = mkT("xbc", 16); T_ysc = mkT("ysc", 8); T_a = mkT("a", KF)
    tokf = [sb("tokf%d" % i, [128, D]) for i in range(6)]; T_tokf = mkT("tokf", 6)
    tokb = [sb("tokb%d" % i, [128, D], BF16) for i in range(8)]; T_tokb = mkT("tokb", 8)
    btok = [sb("btok%d" % i, [128, 512], BF16) for i in range(2)]; T_btok = mkT("btok", 2)
    BTb = [sb("BTb%d" % i, [128, 512], BF16) for i in range(2)]; T_BTb = mkT("BTb", 2)
    CTb = [sb("CTb%d" % i, [128, 512], BF16) for i in range(2)]; T_CTb = mkT("CTb", 2)
    Gm = sb("Gm", [128, 512]); T_Gm = T("Gm")
    seg = sb("seg", [128, 512]); T_seg = T("seg")
    Eb = sb("Eb", [128, 512]); T_E = T("E")
    MT = sb("MT", [128, NH * 128], BF16); T_MT = mkT("MT", 4)
    accA = sb("accA", [128, 384]); T_accA = T("accA")
    accB = sb("accB", [128, 384]); T_accB = T("accB")
    cbuf = sb("cbuf", [128, WMAX]); T_cbuf = T("cbuf")
    cvb = sb("cvb", [128, WMAX]); T_cv = T("cv")
    sgb = sb("sgb", [128, 384]); T_sg = T("sg")
    accC = sb("accC", [128, 384]); T_accC = T("accC")
    accD = sb("accD", [128, 384]); T_accD = T("accD")
    sgb2 = sb("sgb2", [128, 384]); T_sg2 = T("sg2")
    nTst = sb("nTst", [128, KD, 128], BF16); T_nTst = T("nTst")
    ynT = sb("ynT", [128, KD, 128], BF16); T_ynT = T("ynT")
    sm = sb("sm", [128, 16, 32])
    T_sm = mkT("sm", 16)
    dts = [sb("dts%d" % i, [128, 32]) for i in range(2)]; T_dts = mkT("dts", 2)
    qTs = sb("qTs", [128, 128]); T_qTs = T("qTs")
    ss = sb("ss", [128, 8]); T_ss = T("ss")
    rs = sb("rs", [128, 8]); T_rs = T("rs")

    def ps(name, shape, dt=F32):
        return nc.alloc_psum_tensor("p_" + name, list(shape), dt).ap()

    proj = [ps("proj%d" % i, [128, 512]) for i in range(2)]; T_proj = mkT("proj", 2)
    pmix = ps("pmix", [128, 512])
    T_pmix = T("pmix")
    psm = pmix[:, 0:256]; T_psm = [T_pmix] * 4
    pbtr = pmix[:, 256:512].bitcast(BF16); T_pbtr = T_pmix
    assert tuple(pbtr.shape) == (128, 512), pbtr.shape
    ptok = [ps("ptok%d" % i, [128, 512]) for i in range(2)]; T_ptok = mkT("ptok", 2)
    pxtr = ps("pxtr", [128, 1024], BF16); T_pxtr = T("pxtr")
    pgt = ps("pgt", [128, 512]); T_pgt = T("pgt")
    pqbc = ps("pqbc", [128, 512]); T_pqbc = T("pqbc")

    c.dma("sp", consts, consts_d, [], [T_const], part=True)
    c.dma("sp", sel, sel_d, [], [T_const], part=True)
    for l in range(DEPTH):
        c.dma("sp", pv[l], pvec_d[l], [], [T_pv], part=True)
        c.dma("sp", bv[l], bass.AP(bvec_t, l * NBV, [[0, 128], [1, NBV]]), [], [T_pv], part=True)
    c.dma("sp", fnw, bass.AP(fnw_t, 0, [[0, 128], [1, D]]), [], [T_pv], part=True)
    c.copy("dve", [T_const], [T_idb], idb, ident_f)
    c.op("pool", [], [T_zcol], lambda e: e.memset(zcol, 0.0))
    for l in range(DEPTH):
        c.act([T_pv], [T_A], out=Abc[l], in_=bv[l][:, BV_ALOG:BV_ALOG + 32], func=AF.Exp)
        c.ts("dve", [T_A], [T_A], Abc[l], Abc[l], -1.0, ALU.mult)
    segs = [(OFF_ZDT, OFF_WO), (OFF_WO, OFF_WI), (OFF_WI, OFF_WU), (OFF_WU, OFF_WD), (OFF_WD, XT)]
    CB = 4096
    for l in range(DEPTH):
        for si, (a0, a1) in enumerate(segs):
            for b0 in range(a0, a1, CB):
                b1 = min(b0 + CB, a1)
                c.dma("pool", wbf[l, :, b0:b1], wall_d[l, :, b0:b1], [], [T_wseg[l][si]], part=True)
    for l in range(DEPTH):
        for s in range(NSEQ):
            for (buf, Tb) in ((nT_s, T_nTb), (n2T_s, T_n2Tb)):
                c.dma("sp", buf[l][s][:, :, 0:1], zcol, [T_zcol], [Tb[l][s]], part=True, slow=True)
                c.dma("sp", buf[l][s][:, :, L + 1:L + 2], zcol, [T_zcol], [Tb[l][s]], part=True, slow=True)

    ring_i = [0]

    def ring_load(l, si, off):
        i = ring_i[0] % NRING
        ring_i[0] += 1
        c.dma("sp", ring[i].rearrange("p a b -> p (a b)"), wbf[l, :, off:off + 1024], [T_wseg[l][si]], [T_ring[i]])
        return ring[i], T_ring[i]

    T_ss2, T_rs2 = T("ss2"), T("rs2")

    def rstd_from_ss(ncols, inv_n, c0=0, Tss=None, Trs=None):
        Tss = Tss or T_ss
        Trs = Trs or T_rs
        c.act([Tss], [Trs], out=rs[:, c0:c0 + ncols], in_=ss[:, c0:c0 + ncols], func=AF.Ln, scale=inv_n, bias=EPS)
        c.act([Trs], [Trs], out=rs[:, c0:c0 + ncols], in_=rs[:, c0:c0 + ncols], func=AF.Exp, scale=-0.5)

    def norm_to_nT(l, hbuf, Th, nwoff, dst, Tdst, cidx, junk, Tjunk, nbb, Tnb, col=0, Tss=None, Trs=None):
        Tss = Tss or T_ss
        Trs = Trs or T_rs
        c.act([Th], [Tjunk, Tss], out=junk, in_=hbuf, func=AF.Square, accum_out=ss[:, col:col + 1])
        yield
        rstd_from_ss(1, 1.0 / D, col, Tss, Trs)
        yield
        c.ts("dve", [Th, Trs], [Tnb], nbb, hbuf, rs[:, col:col + 1], ALU.mult)
        yield
        c.tr([Tnb, T_idb], [T_pxtr], [(pxtr[:, k * 128:(k + 1) * 128], nbb[:, k * 128:(k + 1) * 128]) for k in range(KD)], idb)
        yield
        c.tt("dve", [T_pxtr, T_pv], [T_nTst], nTst, pxtr.rearrange("p (a b) -> p a b", a=KD),
             bc3(pv[l][:, nwoff:nwoff + KD], KD, 128), ALU.mult)
        c.dma("sp", dst[:, :, 1 + cidx * 128:1 + (cidx + 1) * 128], nTst, [T_nTst], [Tdst])
        yield

    def P1(l, s):
        def load(cidx):
            hb, Thb = tokf[2 + cidx % 2], T_tokf[2 + cidx % 2]
            if l == 0:
                c.dma("sp", hb, x_d[s, cidx * 128:(cidx + 1) * 128, :], [T_x], [Thb])
            else:
                c.dma("sp", hb, ho_s[l - 1][s][cidx], [T_scr["ho"][l - 1][s][cidx]], [Thb])

        load(0)
        yield
        for cidx in range(NCH):
            hb, Thb = tokf[2 + cidx % 2], T_tokf[2 + cidx % 2]
            if cidx + 1 < NCH:
                load(cidx + 1)
            yield from norm_to_nT(l, hb, Thb, PV_N1, nT_s[l][s], T_nT[l][s][cidx], cidx, tokb[5], T_tokb[5], tokb[6], T_tokb[6],
                                  col=4, Tss=T_ss2, Trs=T_rs2)

    def small_chain(l, dtt, Tdt, d0, strictU):
        A = Abc[l]
        a_, cs_, q_, dte_, scl_, dec_, cdte_, tmp_ = (sm[:, i, :] for i in range(5, 13))
        sl = slice(d0, d0 + 16)
        c.tt("dve", [Tdt, T_A], [T_sm[5]], a_, dtt, A, ALU.mult)
        c.mm([T_sm[5], T_const], [T_psm[1]], [(psm[:, 32:64], Uincl, a_, True, True)])
        c.copy("dve", [T_psm[1]], [T_sm[6]], cs_, psm[:, 32:64])
        c.mm([T_sm[6], T_const], [T_psm[2]], [(psm[:, 64:96], Sel127, cs_, True, True)])
        if d0 == 0:
            alhs = a_[:, 0:16]
            Talhs = T_sm[5]
        else:
            alhs = sm[:, 13, 0:16]
            Talhs = T_sm[13]
            c.copy("dve", [T_sm[5]], [T_sm[13]], alhs, a_[:, sl])
        c.mm([Talhs, T_const], [T_psm[3]], [(psm[0:16, 128:256], alhs, Ustrict if strictU else Uincl, True, True)])
        c.copy("dve", [T_psm[3]], [T_qTs], qTs[0:16, :], psm[0:16, 128:256])
        tot_ = sm[:, 14, :]
        c.copy("dve", [T_psm[2]], [T_sm[14]], tot_, psm[:, 64:96])
        if not strictU:
            c.copy("dve", [T_sm[6]], [T_sm[7]], q_, cs_)
            c.tt("dve", [T_sm[14], T_sm[6]], [T_sm[12]], tmp_, tot_, cs_, ALU.subtract)
            c.act([T_sm[12]], [T_sm[8]], out=dte_, in_=tmp_, func=AF.Exp)
            c.act([T_sm[6]], [T_sm[9]], out=scl_, in_=cs_, func=AF.Exp)
        else:
            import os as _os
            k4 = int(_os.environ.get("KSTOP4", "99"))
            c.tt("dve", [T_sm[6], T_sm[5]], [T_sm[7]], q_, cs_, a_, ALU.subtract)
            c.act([T_sm[7]], [T_sm[8]], out=dte_, in_=q_, func=AF.Exp)
            if k4 < 3: return None, None, None, None
            c.tt("dve", [T_sm[14], T_sm[7]], [T_sm[12]], tmp_, tot_, q_, ALU.subtract)
            if k4 < 4: return None, None, None, None
            c.act([T_sm[12]], [T_sm[9]], out=scl_, in_=tmp_, func=AF.Exp)
            if k4 < 5: return None, None, None, None
        c.act([T_sm[14]], [T_sm[10]], out=dec_, in_=tot_, func=AF.Exp)
        if strictU and k4 < 6: return None, None, None, None
        c.tt("dve", [Tdt, T_sm[8]], [T_sm[11]], cdte_, dtt, dte_, ALU.mult)
        return q_[:, sl], scl_[:, sl], dec_[:, sl], cdte_[:, sl]

    def ssd_dir(l, fwd, xtok, Txtok, dtt, Tdt, BTap, TBT, CTap, TCT, btk, Tbtk, Hst, THst, Hbf, THbf, emit_y, fillers=None):
        d0 = 0 if fwd else 16
        q, scl, dec, cdte = small_chain(l, dtt, Tdt, d0, strictU=not fwd)
        import os as _os
        st3 = _os.environ.get("KSTOP3", "") if not fwd else ""
        if st3 == "chain":
            return
        xdt, Txdt = tokb[4], T_tokb[4]
        xdte, Txdte = tokb[5], T_tokb[5]
        x3 = xtok.rearrange("p (a b) -> p a b", a=NH)
        c.tt("pool", [Txtok, Tdt], [Txdt], xdt.rearrange("p (a b) -> p a b", a=NH), x3, bc3(dtt[:, d0:d0 + 16], NH, HP), ALU.mult)
        c.tt("pool", [Txtok, T_sm[11]], [Txdte], xdte.rearrange("p (a b) -> p a b", a=NH), x3, bc3(cdte, NH, HP), ALU.mult)
        c.tt("pool", [T_sm[10]], [THst], Hst.rearrange("p (a b) -> p a b", a=NH), Hst.rearrange("p (a b) -> p a b", a=NH), bc3(dec, NH, HP), ALU.mult)
        c.mm(TBT + TCT, [T_pgt], [(pgt[:, g * 128:(g + 1) * 128], BTap(g), CTap(g), True, True) for g in range(NG)])
        c.tt("dve", [T_pgt, T_const], [T_Gm], Gm.rearrange("p (a b) -> p a b", a=NG), pgt.rearrange("p (a b) -> p a b", a=NG),
             (maskF if fwd else maskB).unsqueeze(1).to_broadcast([128, NG, 128]), ALU.mult)
        if st3 == "gm":
            return
        qring = [(pqbc, T_pqbc), (proj[0], T_proj[0]), (proj[1], T_proj[1])]
        for hg in range(NG):
            qb, Tqb = qring[hg % 3]
            c.mm([T_qTs, T_const], [Tqb], [(qb[:, i * 128:(i + 1) * 128], sel[0:16, (4 * hg + i) * 128:(4 * hg + i + 1) * 128], qTs[0:16, :], True, True) for i in range(4)])
            for i in range(4):
                h = 4 * hg + i
                c.ts("dve", [Tqb, T_sm[7]], [T_seg], seg[:, i * 128:(i + 1) * 128], qb[:, i * 128:(i + 1) * 128],
                     q[:, h:h + 1], ALU.subtract, 0.0, ALU.min if fwd else ALU.max)
            c.act([T_seg], [T_E], out=Eb, in_=seg, func=AF.Exp, scale=1.0 if fwd else -1.0)
            c.tt("dve", [T_E, T_Gm], [T_MT[hg]], MT[:, hg * 512:(hg + 1) * 512].rearrange("p (a b) -> p a b", a=4),
                 Eb.rearrange("p (a b) -> p a b", a=4), Gm[:, hg * 128:(hg + 1) * 128].unsqueeze(1).to_broadcast([128, 4, 128]), ALU.mult)
            if fillers is not None and fillers[hg] is not None:
                fillers[hg]()
        if st3 == "mt":
            return
        for half in range(2):
            c.mm([T_MT[2 * half], T_MT[2 * half + 1], Txdt], [T_ptok[0]],
                 [(ptok[0][:, j * 64:(j + 1) * 64], MT[:, (8 * half + j) * 128:(8 * half + j + 1) * 128],
                   xdt[:, (8 * half + j) * 64:(8 * half + j + 1) * 64], True, True) for j in range(8)])
            c.mm(TCT + [THbf], [T_ptok[1]],
                 [(ptok[1][:, j * 256:(j + 1) * 256], CTap(2 * half + j), Hbf[:, (2 * half + j) * 256:(2 * half + j + 1) * 256], True, True) for j in range(2)])
            emit_y(half, ptok[0], T_ptok[0], ptok[1], T_ptok[1], scl)
        if st3 == "y":
            return
        for half in range(2):
            c.mm([Tbtk, Txdte], [T_ptok[half]],
                 [(ptok[half][:, j * 256:(j + 1) * 256], btk[:, (2 * half + j) * 128:(2 * half + j + 1) * 128],
                   xdte[:, (2 * half + j) * 256:(2 * half + j + 1) * 256], True, True) for j in range(2)])
            c.tt("dve", [T_ptok[half]], [THst], Hst[:, half * 512:(half + 1) * 512], Hst[:, half * 512:(half + 1) * 512], ptok[half], ALU.add)
        c.copy("act", [THst], [THbf], Hbf, Hst)

    def S1(l, s):
        pvl, bvl = pv[l], bv[l]
        for t in T_xbc + T_ysc:
            c.inherit(t, T_a)
        c.dma("sp", wA[:, 0:KD * ZW], wbf[l, :, OFF_ZDT:OFF_ZDT + KD * ZW], [T_wseg[l][0]], [T_wA])
        c.dma("sp", wO, wbf[l, :, OFF_WO + 8 * 1024:OFF_WO + 16 * 1024], [T_wseg[l][1]], [T_wO])
        wz = wA[:, 0:KD * ZW].rearrange("p (a b) -> p a b", a=KD)
        wo3 = wO.rearrange("p (a b) -> p a b", a=8)
        Hst, THst, Hbf, THbf = tokf[2], T_tokf[2], tokb[6], T_tokb[6]
        c.op("pool", [], [THst], lambda e: e.memset(Hst, 0.0))
        c.op("pool", [], [THbf], lambda e: e.memset(Hbf, 0.0))
        pj = [0]
        pring1 = [(proj[0], T_proj[0]), (proj[1], T_proj[1]), (pgt, T_pgt), (pqbc, T_pqbc)]
        for ti, (c0, nch) in enumerate(tiles):
            ncol = nch * 128
            W = ncol + 2
            nw, Tnw = nTw[ti % 2], T_nTw[ti % 2]
            rd = [T_nT[l][s][cc] for cc in range(max(c0 - 1, 0), min(c0 + nch + 1, NCH))] + [T_nTb[l][s]]
            c.dma("sp", nw[:, :, 0:W], nT_s[l][s][:, :, c0 * 128:c0 * 128 + W], rd, [Tnw])
            for fc in range(NFI):
                wt, Twt = ring_load(l, 2, OFF_WI + fc * 1024)
                pb, Tpb = pring1[pj[0] % 4]
                pj[0] += 1
                c.mm([Twt, Tnw], [Tpb], [(pb[:, 0:W], wt[:, k, :], nw[:, k, 0:W], k == 0, k == KD - 1) for k in range(KD)])
                if fc < 16:
                    w0 = pvl[:, PV_CWX + fc * 3 + 0:PV_CWX + fc * 3 + 1]
                    w1 = pvl[:, PV_CWX + fc * 3 + 1:PV_CWX + fc * 3 + 2]
                    w2 = pvl[:, PV_CWX + fc * 3 + 2:PV_CWX + fc * 3 + 3]
                    bb = pvl[:, PV_CBX + fc:PV_CBX + fc + 1]
                    acc, Tacc = (accA, T_accA) if fc % 2 == 0 else (accB, T_accB)
                    c.act([Tpb, T_pv], [Tacc], out=acc[:, 0:ncol], in_=pb[:, 1:1 + ncol], func=AF.Identity, scale=w1, bias=bb)
                    c.stt([Tpb, T_pv, Tacc], [Tacc], acc[:, 0:ncol], pb[:, 0:ncol], w0, acc[:, 0:ncol], ALU.mult, ALU.add)
                    c.stt([Tpb, T_pv, Tacc], [Tacc], acc[:, 0:ncol], pb[:, 2:2 + ncol], w2, acc[:, 0:ncol], ALU.mult, ALU.add)
                    c.act([Tacc], [T_xbc[fc]], out=xbc[:, fc, 0:ncol], in_=acc[:, 0:ncol], func=AF.Silu)
                else:
                    j, r = divmod(fc - 16, 3)
                    if r == 0:
                        c.copy("act", [Tpb], [T_cbuf], cbuf[:, 0:W], pb[:, 0:W])
                    elif r == 1:
                        w0 = pvl[:, PV_CWS + j * 3 + 0:PV_CWS + j * 3 + 1]
                        w1 = pvl[:, PV_CWS + j * 3 + 1:PV_CWS + j * 3 + 2]
                        w2 = pvl[:, PV_CWS + j * 3 + 2:PV_CWS + j * 3 + 3]
                        c.tt("dve", [Tpb, T_cbuf], [T_cv], cvb[:, 0:W], pb[:, 0:W], cbuf[:, 0:W], ALU.mult)
                        c.act([T_cv, T_pv], [T_sg], out=sgb[:, 0:ncol], in_=cvb[:, 1:1 + ncol], func=AF.Copy, scale=w1)
                        c.stt([T_cv, T_pv, T_sg], [T_sg], sgb[:, 0:ncol], cvb[:, 0:ncol], w0, sgb[:, 0:ncol], ALU.mult, ALU.add)
                        c.stt([T_cv, T_pv, T_sg], [T_sg], sgb[:, 0:ncol], cvb[:, 2:2 + ncol], w2, sgb[:, 0:ncol], ALU.mult, ALU.add)
                    else:
                        c.tt("dve", [Tpb, T_sg], [T_ysc[j]], ysc[:, j, 0:ncol], pb[:, 1:1 + ncol], sgb[:, 0:ncol], ALU.mult)
            def zdt_mm(jc, part):
                cidx = c0 + jc
                ns0 = 1 + jc * 128
                szb, Tsz = tokb[cidx % 2], T_tokb[cidx % 2]
                dtt, Tdt = dts[cidx % 2], T_dts[cidx % 2]
                if part == 0:
                    for hh in range(2):
                        c.mm([Tnw, T_wA], [T_ptok[hh]],
                             [(ptok[hh], nw[:, k, ns0:ns0 + 128], wz[:, k, hh * 512:(hh + 1) * 512], k == 0, k == KD - 1) for k in range(KD)])
                        c.act([T_ptok[hh]], [Tsz], out=szb[:, hh * 512:(hh + 1) * 512], in_=ptok[hh], func=AF.Silu)
                    c.dma("sp", sz_s[l][s][cidx], szb, [Tsz], [T_scr["sz"][l][s][cidx]])
                else:
                    c.mm([Tnw, T_wA], [T_psm[0]],
                         [(psm[:, 0:32], nw[:, k, ns0:ns0 + 128], wz[:, k, 1024:1056], k == 0, k == KD - 1) for k in range(KD)])
                    xb_, m_, e_, l_ = (sm[:, i, :] for i in range(0, 4))
                    c.tt("dve", [T_psm[0], T_pv], [T_sm[0]], xb_, psm[:, 0:32], bvl[:, BV_DTB:BV_DTB + 32], ALU.add)
                    c.ts("dve", [T_sm[0]], [T_sm[1]], m_, xb_, 30.0, ALU.min)
                    c.act([T_sm[1]], [T_sm[2]], out=e_, in_=m_, func=AF.Exp)
                    c.act([T_sm[2]], [T_sm[3]], out=l_, in_=e_, func=AF.Ln, bias=1.0)
                    c.tt("dve", [T_sm[3], T_sm[0]], [Tdt], dtt, l_, xb_, ALU.max)
                    c.dma("sp", dt_s[l][s][cidx], dtt, [Tdt], [T_scr["dt"][l][s][cidx]])

            def sc_out(jc, half, hb, Thb, hpb, Thp):
                cs0 = jc * 128
                c.mm(T_ysc + [T_wO], [T_ptok[half]],
                     [(ptok[half], ysc[:, k, cs0:cs0 + 128], wo3[:, k, half * 512:(half + 1) * 512], k == 0, k == 7) for k in range(8)])
                c.tt("dve", [T_ptok[half], Thb], [Thp], hpb[:, half * 512:(half + 1) * 512], hb[:, half * 512:(half + 1) * 512], ptok[half], ALU.add)
                if half == 1:
                    c.dma("sp", hp_s[l][s][c0 + jc], hpb, [Thp], [T_scr["hp"][l][s][c0 + jc]])

            zdt_mm(0, 0)
            zdt_mm(0, 1)
            for jc in range(nch):
                cidx = c0 + jc
                cs0 = jc * 128
                dtt, Tdt = dts[cidx % 2], T_dts[cidx % 2]
                hb, Thb = tokf[cidx % 2], T_tokf[cidx % 2]
                if l == 0:
                    c.dma("sp", hb, x_d[s, cidx * 128:(cidx + 1) * 128, :], [T_x], [Thb])
                else:
                    c.dma("sp", hb, ho_s[l - 1][s][cidx], [T_scr["ho"][l - 1][s][cidx]], [Thb])
                hpb, Thp = tokf[4], T_tokf[4]
                c.tr([T_xbc[k] for k in range(8)] + [T_idb], [T_pxtr],
                     [(pxtr[:, k * 128:(k + 1) * 128], xbc[:, k, cs0:cs0 + 128]) for k in range(8)], idb)
                xtok, Txtok = tokb[2], T_tokb[2]
                c.copy("act", [T_pxtr], [Txtok], xtok, pxtr)
                c.dma("sp", xt_s[l][s][cidx], xtok, [Txtok], [T_scr["xt"][l][s][cidx]])
                c.tr([T_xbc[8 + g] for g in range(4)] + [T_idb], [T_pbtr],
                     [(pbtr[:, g * 128:(g + 1) * 128], xbc[:, 8 + g, cs0:cs0 + 128]) for g in range(4)], idb)
                btk, Tbtk = btok[0], T_btok[0]
                c.copy("dve", [T_pbtr], [Tbtk], btk, pbtr)
                c.dma("sp", bt_s[l][s][cidx], btk, [Tbtk], [T_scr["bt"][l][s][cidx]])
                TB4 = [T_xbc[8 + g] for g in range(4)]
                TC4 = [T_xbc[12 + g] for g in range(4)]
                c.dma("sp", BT_s[l][s][cidx].rearrange("p (a b) -> p a b", a=4), xbc[:, 8:12, cs0:cs0 + 128], TB4, [T_scr["BT"][l][s][cidx]])
                c.dma("sp", CT_s[l][s][cidx].rearrange("p (a b) -> p a b", a=4), xbc[:, 12:16, cs0:cs0 + 128], TC4, [T_scr["CT"][l][s][cidx]])
                ypb, Typ = tokf[3], T_tokf[3]
                t1, Tt1 = tokf[5], T_tokf[5]

                def emit_y(half, dg, Tdg, of, Tof, scl, xtok=xtok, Txtok=Txtok, ypb=ypb, Typ=Typ, t1=t1, Tt1=Tt1):
                    hs = slice(half * 512, (half + 1) * 512)
                    t1h = t1[:, hs].rearrange("p (a b) -> p a b", a=8)
                    c.tt("dve", [Tof, T_sm[9]], [Tt1], t1h, of.rearrange("p (a b) -> p a b", a=8), bc3(scl[:, 8 * half:8 * half + 8], 8, HP), ALU.mult)
                    c.tt("dve", [Tdg, Tt1], [Tt1], t1[:, hs], dg, t1[:, hs], ALU.add)
                    yph = ypb[:, hs].rearrange("p (a b) -> p a b", a=8)
                    c.tt("pool", [Txtok, T_pv], [Typ], yph, xtok[:, hs].rearrange("p (a b) -> p a b", a=8),
                         bc3(bvl[:, BV_D + 8 * half:BV_D + 8 * half + 8], 8, HP), ALU.mult)
                    c.tt("dve", [Tt1, Typ], [Typ], ypb[:, hs], ypb[:, hs], t1[:, hs], ALU.add)

                nxt = jc + 1 < nch
                fillers = [
                    (lambda jc=jc, hb=hb, Thb=Thb, hpb=hpb, Thp=Thp: sc_out(jc, 0, hb, Thb, hpb, Thp)),
                    (lambda jc=jc, hb=hb, Thb=Thb, hpb=hpb, Thp=Thp: sc_out(jc, 1, hb, Thb, hpb, Thp)),
                    (lambda jc=jc: zdt_mm(jc + 1, 0)) if nxt else None,
                    (lambda jc=jc: zdt_mm(jc + 1, 1)) if nxt else None,
                ]
                ssd_dir(l, True, xtok, Txtok, dtt, Tdt,
                        lambda g: xbc[:, 8 + g, cs0:cs0 + 128], TB4, lambda g: xbc[:, 12 + g, cs0:cs0 + 128], TC4,
                        btk, Tbtk, Hst, THst, Hbf, THbf, emit_y, fillers=fillers)
                c.dma("sp", yp_s[l][s][cidx], ypb, [Typ], [T_scr["yp"][l][s][cidx]])

    def S2(l, s, last):
        pvl, bvl = pv[l], bv[l]
        c.dma("sp", wO, wbf[l, :, OFF_WO:OFF_WO + 8 * 1024], [T_wseg[l][1]], [T_wO])
        wo3 = wO.rearrange("p (a b) -> p a b", a=8)
        Hst, THst, Hbf, THbf = tokf[2], T_tokf[2], tokb[6], T_tokb[6]
        c.op("pool", [], [THst], lambda e: e.memset(Hst, 0.0))
        c.op("pool", [], [THbf], lambda e: e.memset(Hbf, 0.0))
        normw = bvl[:, BV_NW:BV_NW + 1024]
        pending = None
        for it, cidx in enumerate(reversed(range(NCH))):
            p2 = it % 2
            ypb, Typ = tokf[p2], T_tokf[p2]
            hpb, Thp = tokf[3 + p2], T_tokf[3 + p2]
            szb, Tsz = tokb[p2], T_tokb[p2]
            xtok, Txtok = tokb[2 + p2], T_tokb[2 + p2]
            btk, Tbtk = btok[p2], T_btok[p2]
            BTt, TBTt = BTb[p2], T_BTb[p2]
            CTt, TCTt = CTb[p2], T_CTb[p2]
            dtt, Tdt = dts[p2], T_dts[p2]
            TS = lambda nm: T_scr[nm][l][s][cidx]
            c.dma("sp", dtt, dt_s[l][s][cidx], [TS("dt")], [Tdt])
            c.dma("sp", xtok, xt_s[l][s][cidx], [TS("xt")], [Txtok])
            c.dma("sp", BTt, BT_s[l][s][cidx], [TS("BT")], [TBTt])
            c.dma("sp", CTt, CT_s[l][s][cidx], [TS("CT")], [TCTt])
            c.dma("sp", btk, bt_s[l][s][cidx], [TS("bt")], [Tbtk])
            c.dma("sp", ypb, yp_s[l][s][cidx], [TS("yp")], [Typ])
            c.dma("sp", szb, sz_s[l][s][cidx], [TS("sz")], [Tsz])
            c.dma("sp", hpb, hp_s[l][s][cidx], [TS("hp")], [Thp])
            t1, Tt1 = tokf[5], T_tokf[5]
            import os as _os
            st2 = _os.environ.get("KSTOP2", "")
            if st2 == "loads":
                continue

            def emit_y(half, dg, Tdg, of, Tof, scl, ypb=ypb, Typ=Typ, szb=szb, Tsz=Tsz, t1=t1, Tt1=Tt1):
                hs = slice(half * 512, (half + 1) * 512)
                t1h = t1[:, hs].rearrange("p (a b) -> p a b", a=8)
                c.tt("dve", [Tof, T_sm[9]], [Tt1], t1h, of.rearrange("p (a b) -> p a b", a=8), bc3(scl[:, 8 * half:8 * half + 8], 8, HP), ALU.mult)
                c.tt("dve", [Tdg, Tt1], [Tt1], t1[:, hs], dg, t1[:, hs], ALU.add)
                c.tt("dve", [Tt1, Typ], [Tt1], t1[:, hs], t1[:, hs], ypb[:, hs], ALU.add)
                c.tt("dve", [Tt1, Tsz], [Tt1], t1[:, hs], t1[:, hs], szb[:, hs], ALU.mult)

            ssd_dir(l, False, xtok, Txtok, dtt, Tdt,
                    lambda g: BTt[:, g * 128:(g + 1) * 128], [TBTt], lambda g: CTt[:, g * 128:(g + 1) * 128], [TCTt],
                    btk, Tbtk, Hst, THst, Hbf, THbf, emit_y, fillers=pending)

            def make_post(cidx=cidx, hpb=hpb, Thp=Thp, nbb=szb, Tnb=Tsz, t1=t1, Tt1=Tt1):
                junk, Tjunk = tokb[7], T_tokb[7]
                ynb, Tyn = tokb[7], T_tokb[7]

                def p0():
                    for g in range(NG):
                        c.act([Tt1], [Tjunk, T_ss], out=junk[:, g * 256:(g + 1) * 256], in_=t1[:, g * 256:(g + 1) * 256], func=AF.Square, accum_out=ss[:, g:g + 1])
                    rstd_from_ss(NG, 1.0 / 256)
                    for g in range(NG):
                        c.stt([Tt1, T_rs, T_pv], [Tyn], ynb[:, g * 256:(g + 1) * 256], t1[:, g * 256:(g + 1) * 256], rs[:, g:g + 1],
                              normw[:, g * 256:(g + 1) * 256], ALU.mult, ALU.mult)

                def outp(half):
                    c.mm([T_ynT, T_wO], [T_ptok[half]],
                         [(ptok[half], ynT[:, k, :], wo3[:, k, half * 512:(half + 1) * 512], k == 0, k == 7) for k in range(8)])
                    c.tt("dve", [T_ptok[half], Thp], [Thp], hpb[:, half * 512:(half + 1) * 512], hpb[:, half * 512:(half + 1) * 512], ptok[half], ALU.add)

                def p1():
                    c.tr([Tyn, T_idb], [T_pxtr], [(pxtr[:, k * 128:(k + 1) * 128], ynb[:, k * 128:(k + 1) * 128]) for k in range(KD)], idb)
                    c.copy("act", [T_pxtr], [T_ynT], ynT, pxtr.rearrange("p (a b) -> p a b", a=KD))
                    outp(0)

                def p2():
                    outp(1)
                    c.dma("sp", hm_s[l][s][cidx], hpb, [Thp], [T_scr["hm"][l][s][cidx]])

                def p3():
                    for _ in norm_to_nT(l, hpb, Thp, PV_N2, n2T_s[l][s], T_n2T[l][s][cidx], cidx, tokb[7], T_tokb[7], nbb, Tnb):
                        pass

                return [p0, p1, p2, p3]

            pending = make_post()
        for f in pending:
            f()

    def S3(l, s, last, bg=None):
        pvl = pv[l]
        for t in T_a:
            c.inherit(t, T_xbc + T_ysc)
        c.dma("sp", wA, wbf[l, :, OFF_WD:OFF_WD + KF * 1024], [T_wseg[l][4]], [T_wA])
        wd3 = wA.rearrange("p (a b) -> p a b", a=KF)
        pj = [0]
        pring3 = [(proj[0], T_proj[0]), (proj[1], T_proj[1]), (pgt, T_pgt), (pqbc, T_pqbc)]
        for ti, (c0, nch) in enumerate(tiles):
            ncol = nch * 128
            W = ncol + 2
            nw, Tnw = nTw[ti % 2], T_nTw[ti % 2]
            rd = [T_n2T[l][s][cc] for cc in range(max(c0 - 1, 0), min(c0 + nch + 1, NCH))] + [T_n2Tb[l][s]]
            c.dma("sp", nw[:, :, 0:W], n2T_s[l][s][:, :, c0 * 128:c0 * 128 + W], rd, [Tnw])
            for j in range(KF):
                accs = []
                for r in range(2):
                    fc = 2 * j + r
                    wt, Twt = ring_load(l, 3, OFF_WU + fc * 1024)
                    pb, Tpb = pring3[pj[0] % 4]
                    pj[0] += 1
                    c.mm([Twt, Tnw], [Tpb], [(pb[:, 0:W], wt[:, k, :], nw[:, k, 0:W], k == 0, k == KD - 1) for k in range(KD)])
                    w0 = pvl[:, PV_CWF + fc * 3 + 0:PV_CWF + fc * 3 + 1]
                    w1 = pvl[:, PV_CWF + fc * 3 + 1:PV_CWF + fc * 3 + 2]
                    w2 = pvl[:, PV_CWF + fc * 3 + 2:PV_CWF + fc * 3 + 3]
                    bb = pvl[:, PV_CBF + fc:PV_CBF + fc + 1]
                    if j % 2 == 0:
                        acc, Tacc = (accA, T_accA) if r == 0 else (accB, T_accB)
                        sg_, Tsg_ = sgb, T_sg
                    else:
                        acc, Tacc = (accC, T_accC) if r == 0 else (accD, T_accD)
                        sg_, Tsg_ = sgb2, T_sg2
                    c.act([Tpb, T_pv], [Tacc], out=acc[:, 0:ncol], in_=pb[:, 1:1 + ncol], func=AF.Identity, scale=w1, bias=bb)
                    c.stt([Tpb, T_pv, Tacc], [Tacc], acc[:, 0:ncol], pb[:, 0:ncol], w0, acc[:, 0:ncol], ALU.mult, ALU.add)
                    c.stt([Tpb, T_pv, Tacc], [Tacc], acc[:, 0:ncol], pb[:, 2:2 + ncol], w2, acc[:, 0:ncol], ALU.mult, ALU.add)
                    if r == 0:
                        c.act([Tacc], [Tsg_], out=sg_[:, 0:ncol], in_=acc[:, 0:ncol], func=AF.Silu)
                c.tt("pool", [Tsg_, Tacc], [T_a[j]], abuf[:, j, 0:ncol], sg_[:, 0:ncol], acc[:, 0:ncol], ALU.mult)
                if bg is not None:
                    next(bg, None)
            for jc in range(nch):
                cidx = c0 + jc
                cs0 = jc * 128
                hb, Thb = tokf[cidx % 2], T_tokf[cidx % 2]
                c.dma("sp", hb, hm_s[l][s][cidx], [T_scr["hm"][l][s][cidx]], [Thb])
                for half in range(2):
                    c.mm(T_a + [T_wA], [T_ptok[half]],
                         [(ptok[half], abuf[:, k, cs0:cs0 + 128], wd3[:, k, half * 512:(half + 1) * 512], k == 0, k == KF - 1) for k in range(KF)])
                    c.tt("dve", [T_ptok[half], Thb], [Thb], hb[:, half * 512:(half + 1) * 512], hb[:, half * 512:(half + 1) * 512], ptok[half], ALU.add)
                if not last:
                    c.dma("sp", ho_s[l][s][cidx], hb, [Thb], [T_scr["ho"][l][s][cidx]])
                else:
                    junk, Tjunk = tokb[7], T_tokb[7]
                    ob, Tob = tokf[4 + cidx % 2], T_tokf[4 + cidx % 2]
                    c.act([Thb], [Tjunk, T_ss], out=junk, in_=hb, func=AF.Square, accum_out=ss[:, 0:1])
                    rstd_from_ss(1, 1.0 / D)
                    c.stt([Thb, T_rs, T_pv], [Tob], ob, hb, rs[:, 0:1], fnw, ALU.mult, ALU.mult)
                    c.dma("sp", out_d[s, cidx * 128:(cidx + 1) * 128, :], ob, [Tob], [T_out[s][cidx]])

    units = [(l, s) for l in range(DEPTH) for s in range(NSEQ)]
    pend = P1(*units[0])
    for ui, (l, s) in enumerate(units):
        for _ in pend:
            pass
        S1(l, s)
        S2(l, s, l == DEPTH - 1)
        pend = P1(*units[ui + 1]) if ui + 1 < len(units) else iter(())
        S3(l, s, l == DEPTH - 1, bg=pend)
    for s in range(NSEQ):
        for cc in range(NCH):
            for ev in T_out[s][cc].w.values():
                c._wait("sp", ev)
    if dbg:
        for nm in T_scr:
            for l in range(DEPTH):
                for s in range(NSEQ):
                    for cc in range(NCH):
                        for ev in T_scr[nm][l][s][cc].w.values():
                            c._wait("sp", ev)
    build.stats = (c.nins, c.nwaits)
    return nc


def prep_params(inp, DEPTH):
    f = lambda a: np.ascontiguousarray(np.asarray(a, dtype=np.float32))
    w_in = f(inp["w_in"])

    def pk(w):
        d, r, cc = w.shape
        return w.reshape(d, r // 128, 128, cc).transpose(0, 2, 1, 3)

    cols = [1024 + j * 128 for j in range(8)] + [2048 + j * 128 for j in range(4)] + [2560 + j * 128 for j in range(4)]
    for j in range(8):
        cols += [4128 + j * 128, 5152 + j * 128, 3104 + j * 128]
    wi = np.stack([pk(w_in[:, :, c0:c0 + 128]) for c0 in cols], axis=2).reshape(DEPTH, 128, NFI * 1024)
    wzdt = pk(np.concatenate([w_in[:, :, 0:1024], w_in[:, :, 3072:3104]], axis=2)).reshape(DEPTH, 128, KD * ZW)
    wo = pk(f(inp["w_out"])).reshape(DEPTH, 128, 16 * 1024)
    w_up = f(inp["w_ffn_up"])
    ucols = []
    for j in range(KF):
        ucols += [j * 128, 2816 + j * 128]
    wu = np.stack([pk(w_up[:, :, c0:c0 + 128]) for c0 in ucols], axis=2).reshape(DEPTH, 128, NFU * 1024)
    wd = pk(f(inp["w_ffn_down"])).reshape(DEPTH, 128, KF * 1024)
    wall = np.ascontiguousarray(np.concatenate([wzdt, wo, wi, wu, wd], axis=2))
    assert wall.shape == (DEPTH, 128, XT)

    pvec = np.zeros((DEPTH, 128, NPV), np.float32)
    uidx = np.concatenate([np.arange(c0, c0 + 128) for c0 in ucols])
    for l in range(DEPTH):
        pvec[l, :, PV_CWX:PV_CWX + 48] = f(inp["conv_xbc_w"])[l].reshape(3, 16, 128).transpose(2, 1, 0).reshape(128, 48)
        pvec[l, :, PV_CBX:PV_CBX + 16] = f(inp["conv_xbc_b"])[l].reshape(16, 128).T
        pvec[l, :, PV_CWS:PV_CWS + 24] = f(inp["sc_conv_w"])[l].reshape(3, 8, 128).transpose(2, 1, 0).reshape(128, 24)
        pvec[l, :, PV_CWF:PV_CWF + 132] = f(inp["ffn_conv_w"])[l][:, uidx].reshape(3, NFU, 128).transpose(2, 1, 0).reshape(128, 132)
        pvec[l, :, PV_CBF:PV_CBF + 44] = f(inp["ffn_conv_b"])[l][uidx].reshape(NFU, 128).T
        pvec[l, :, PV_N1:PV_N1 + 8] = f(inp["norm1_w"])[l].reshape(8, 128).T
        pvec[l, :, PV_N2:PV_N2 + 8] = f(inp["norm2_w"])[l].reshape(8, 128).T
    bvec = np.zeros((DEPTH, NBV), np.float32)
    for l in range(DEPTH):
        bvec[l, BV_NW:BV_NW + 1024] = f(inp["ssd_norm_w"])[l]
        bvec[l, BV_D:BV_D + 16] = f(inp["d_skip"])[l]
        bvec[l, BV_DTB:BV_DTB + 16] = f(inp["dt_bias_f"])[l]
        bvec[l, BV_DTB + 16:BV_DTB + 32] = f(inp["dt_bias_b"])[l]
        bvec[l, BV_ALOG:BV_ALOG + 16] = f(inp["a_log_f"])[l]
        bvec[l, BV_ALOG + 16:BV_ALOG + 32] = f(inp["a_log_b"])[l]
    fnw = f(inp["final_norm_w"]).reshape(1, D)
    t = np.arange(128)
    consts = np.zeros((128, 768), np.float32)
    consts[:, 0:128] = np.eye(128)
    consts[:, 128:256] = (t[:, None] <= t[None, :])
    consts[:, 256:384] = (t[:, None] < t[None, :])
    consts[:, 384:512] = (t[None, :] >= t[:, None])
    consts[:, 512:640] = (t[None, :] <= t[:, None])
    consts[127, 640:768] = 1.0
    sel = np.zeros((128, 2048), np.float32)
    for h in range(16):
        sel[h, h * 128:(h + 1) * 128] = 1.0
    return dict(wall=wall, pvec=pvec, bvec=bvec, fnw=fnw, consts=consts, sel=sel)


_NC_CACHE = {}


def kernel(**inputs):
    x = np.ascontiguousarray(np.asarray(inputs["x"], dtype=np.float32))
    B, L, _ = x.shape
    DEPTH = np.asarray(inputs["w_in"]).shape[0]
    ncores = 8
    NSEQ = B // ncores
    params = prep_params(inputs, DEPTH)
    key = (L, NSEQ, DEPTH)
    if key not in _NC_CACHE:
        _NC_CACHE[key] = build(L, NSEQ, DEPTH)
    nc = _NC_CACHE[key]
    in_maps = []
    for i in range(ncores):
        m = dict(params)
        m["x"] = np.ascontiguousarray(x[i * NSEQ:(i + 1) * NSEQ])
        in_maps.append(m)
    res = run_bass_kernel_spmd(nc, in_maps, core_ids=list(range(ncores)))
    out = np.concatenate([np.asarray(r["out"]) for r in res.results], axis=0)
    return out.astype(np.float32, copy=False)
```
